# Optimizing a Trainium2 kernel written in Bass

```python
import math
import jax, jax.numpy as jnp
from jax import lax
import numpy as np

D_MODEL = 2048
BATCH = 4
SEQ = 4096
DEPTH = 2

N_MIXERS = 2
N_CONV_LAYERS = (DEPTH + 1) // 2
N_DN_LAYERS = DEPTH // 2

CONV_KERNEL = 31

DN_HEAD_K = 128
DN_HEAD_V = 128
DN_NUM_K_HEADS = D_MODEL // DN_HEAD_K
DN_NUM_V_HEADS = 2 * DN_NUM_K_HEADS
DN_KEY_DIM = DN_NUM_K_HEADS * DN_HEAD_K
DN_VALUE_DIM = DN_NUM_V_HEADS * DN_HEAD_V
DN_QKV_DIM = 2 * DN_KEY_DIM + DN_VALUE_DIM
DN_IN_DIM = DN_QKV_DIM + DN_VALUE_DIM + 4 * DN_NUM_V_HEADS
DN_SHORT_CONV = 5
CHUNK = 64

FFN_HIDDEN = ((8 * D_MODEL // 3 + 255) // 256) * 256

RMS_EPS = 1e-6
LN_EPS = 1e-5

kernel_name = "hybrid_conformer_gdn_encoder"


def rmsnorm(x, w, eps=RMS_EPS):
    xf = x.astype(jnp.float32)
    y = xf * lax.rsqrt(jnp.mean(xf * xf, axis=-1, keepdims=True) + eps)
    return (y * w.astype(jnp.float32)).astype(x.dtype)


def layernorm(x, g, b, eps=LN_EPS):
    xf = x.astype(jnp.float32)
    mu = jnp.mean(xf, axis=-1, keepdims=True)
    xc = xf - mu
    var = jnp.mean(xc * xc, axis=-1, keepdims=True)
    return (xc * lax.rsqrt(var + eps) * g.astype(jnp.float32) + b.astype(jnp.float32)).astype(x.dtype)


def l2norm(x, eps=1e-6):
    xf = x.astype(jnp.float32)
    return xf * lax.rsqrt(jnp.sum(xf * xf, axis=-1, keepdims=True) + eps)


def depthwise_conv_centred(x, w):
    width = w.shape[0]
    pad = (width - 1) // 2
    return lax.conv_general_dilated(
        x, w[:, None, :].astype(x.dtype), window_strides=(1,), padding=[(pad, width - 1 - pad)],
        dimension_numbers=("NWC", "WIO", "NWC"), feature_group_count=x.shape[-1])


def conformer_conv_module(h, w_pw1, b_pw1, w_dw, b_dw, ln_g, ln_b, w_pw2, b_pw2):
    u = h @ w_pw1 + b_pw1
    a, gate = jnp.split(u, 2, axis=-1)
    u = a * jax.nn.sigmoid(gate)
    u = depthwise_conv_centred(u, w_dw) + b_dw
    u = jax.nn.silu(layernorm(u, ln_g, ln_b))
    return u @ w_pw2 + b_pw2


def chunk_gated_delta_rule(q, k, v, g, beta):
    B, S, H, Dk = q.shape
    Dv = v.shape[-1]
    NC = S // CHUNK

    def to_chunks(t):
        return t.reshape(B, NC, CHUNK, H, t.shape[-1]).transpose(1, 0, 3, 2, 4)

    q = to_chunks(q.astype(jnp.float32)) * (Dk ** -0.5)
    k = to_chunks(k.astype(jnp.float32))
    v = to_chunks(v.astype(jnp.float32))
    beta = beta.astype(jnp.float32).reshape(B, NC, CHUNK, H).transpose(1, 0, 3, 2)
    g = jnp.cumsum(g.astype(jnp.float32).reshape(B, NC, CHUNK, H).transpose(1, 0, 3, 2), axis=-1)

    tril = jnp.tril(jnp.ones((CHUNK, CHUNK), dtype=bool))
    strict = jnp.tril(jnp.ones((CHUNK, CHUNK), dtype=bool), k=-1)
    diff = g[..., :, None] - g[..., None, :]
    decay = jnp.where(tril, jnp.exp(jnp.where(tril, diff, 0.0)), 0.0)

    kk = jnp.einsum("nbhcd,nbhed->nbhce", k, k)
    lower = jnp.where(strict, kk * decay * beta[..., :, None], 0.0)
    eye = jnp.eye(CHUNK, dtype=jnp.float32)
    tmat = lower + eye
    rhs = jnp.concatenate([v * beta[..., None], k * (beta * jnp.exp(g))[..., None]], axis=-1)
    sol = lax.linalg.triangular_solve(tmat, rhs, left_side=True, lower=True, unit_diagonal=True)
    u, w = sol[..., :Dv], sol[..., Dv:]

    a_qk = jnp.where(tril, jnp.einsum("nbhcd,nbhed->nbhce", q, k) * decay, 0.0)
    g_last = g[..., -1:]
    q_g = q * jnp.exp(g)[..., None]
    k_g = k * jnp.exp(g_last - g)[..., None]
    decay_last = jnp.exp(g_last)[..., None]

    def step(state, inp):
        q_c, k_c, u_c, w_c, a_c, d_c = inp
        v_new = u_c - jnp.einsum("bhcd,bhde->bhce", w_c, state)
        o_c = jnp.einsum("bhcd,bhde->bhce", q_c, state) + jnp.einsum("bhce,bhef->bhcf", a_c, v_new)
        state = state * d_c + jnp.einsum("bhcd,bhce->bhde", k_c, v_new)
        return state, o_c

    state0 = jnp.zeros((B, H, Dk, Dv), dtype=jnp.float32)
    _, o = lax.scan(step, state0, (q_g, k_g, u, w, a_qk, decay_last))
    return o.transpose(1, 0, 3, 2, 4).reshape(B, S, H, Dv)


def gated_deltanet_bidir(h, w_in, w_conv, a_log, dt_bias, norm_w, w_out):
    B, S, _ = h.shape
    proj = h @ w_in
    qkv = proj[..., :DN_QKV_DIM]
    z = proj[..., DN_QKV_DIM:DN_QKV_DIM + DN_VALUE_DIM]
    beta_raw = proj[..., DN_QKV_DIM + DN_VALUE_DIM:DN_QKV_DIM + DN_VALUE_DIM + 2 * DN_NUM_V_HEADS]
    a_raw = proj[..., DN_QKV_DIM + DN_VALUE_DIM + 2 * DN_NUM_V_HEADS:]

    qkv = jax.nn.silu(depthwise_conv_centred(qkv, w_conv))
    q = qkv[..., :DN_KEY_DIM].reshape(B, S, DN_NUM_K_HEADS, DN_HEAD_K)
    k = qkv[..., DN_KEY_DIM:2 * DN_KEY_DIM].reshape(B, S, DN_NUM_K_HEADS, DN_HEAD_K)
    v = qkv[..., 2 * DN_KEY_DIM:].reshape(B, S, DN_NUM_V_HEADS, DN_HEAD_V)
    rep = DN_NUM_V_HEADS // DN_NUM_K_HEADS
    q = jnp.repeat(l2norm(q), rep, axis=2)
    k = jnp.repeat(l2norm(k), rep, axis=2)

    beta = jax.nn.sigmoid(beta_raw.astype(jnp.float32)).reshape(B, S, 2, DN_NUM_V_HEADS)
    g = -jnp.exp(a_log.astype(jnp.float32)) * jax.nn.softplus(
        a_raw.astype(jnp.float32).reshape(B, S, 2, DN_NUM_V_HEADS) + dt_bias.astype(jnp.float32))

    o_fwd = chunk_gated_delta_rule(q, k, v, g[:, :, 0], beta[:, :, 0])
    flip = lambda t: jnp.flip(t, axis=1)
    o_bwd = flip(chunk_gated_delta_rule(flip(q), flip(k), flip(v), flip(g[:, :, 1]), flip(beta[:, :, 1])))
    o = o_fwd + o_bwd

    zf = z.astype(jnp.float32).reshape(B, S, DN_NUM_V_HEADS, DN_HEAD_V)
    o = o * lax.rsqrt(jnp.mean(o * o, axis=-1, keepdims=True) + RMS_EPS) * norm_w.astype(jnp.float32) * jax.nn.silu(zf)
    return o.reshape(B, S, DN_VALUE_DIM).astype(h.dtype) @ w_out


def swiglu_ffn(h, w_gate_up, w_down):
    gu = h @ w_gate_up
    gate, up = gu[..., :FFN_HIDDEN], gu[..., FFN_HIDDEN:]
    return (jax.nn.silu(gate) * up) @ w_down


def setup_inputs(seed: int = 0) -> dict:
    key = jax.random.key(seed)
    ks = jax.random.split(key, 24)
    f32 = jnp.float32
    D = D_MODEL

    def normal(k, shape, scale):
        return jax.random.normal(k, shape, dtype=f32) * scale

    x = jax.random.normal(ks[0], (BATCH, SEQ, D), dtype=f32)
    mix_norm = 1.0 + normal(ks[1], (DEPTH, D), 0.02)
    ffn_norm = 1.0 + normal(ks[2], (DEPTH, D), 0.02)
    final_norm = 1.0 + normal(ks[3], (D,), 0.02)

    cv_w_pw1 = normal(ks[4], (N_CONV_LAYERS, D, 2 * D), D ** -0.5)
    cv_b_pw1 = normal(ks[5], (N_CONV_LAYERS, 2 * D), 0.01)
    cv_w_dw = normal(ks[6], (N_CONV_LAYERS, CONV_KERNEL, D), CONV_KERNEL ** -0.5)
    cv_b_dw = normal(ks[7], (N_CONV_LAYERS, D), 0.01)
    cv_ln_g = 1.0 + normal(ks[8], (N_CONV_LAYERS, D), 0.02)
    cv_ln_b = normal(ks[9], (N_CONV_LAYERS, D), 0.01)
    cv_w_pw2 = normal(ks[10], (N_CONV_LAYERS, D, D), D ** -0.5)
    cv_b_pw2 = normal(ks[11], (N_CONV_LAYERS, D), 0.01)

    dn_w_in = normal(ks[12], (N_DN_LAYERS, D, DN_IN_DIM), D ** -0.5)
    dn_w_conv = normal(ks[13], (N_DN_LAYERS, DN_SHORT_CONV, DN_QKV_DIM), DN_SHORT_CONV ** -0.5)
    dn_a_log = jnp.log(jax.random.uniform(ks[14], (N_DN_LAYERS, 2, DN_NUM_V_HEADS), dtype=f32, minval=1.0, maxval=16.0))
    dt = jnp.exp(jax.random.uniform(ks[15], (N_DN_LAYERS, 2, DN_NUM_V_HEADS), dtype=f32,
                                    minval=math.log(1e-3), maxval=math.log(1e-1)))
    dn_dt_bias = dt + jnp.log(-jnp.expm1(-dt))
    dn_norm_w = 1.0 + normal(ks[16], (N_DN_LAYERS, DN_HEAD_V), 0.02)
    dn_w_out = normal(ks[17], (N_DN_LAYERS, DN_VALUE_DIM, D), DN_VALUE_DIM ** -0.5)

    ffn_w_gate_up = normal(ks[18], (DEPTH, D, 2 * FFN_HIDDEN), D ** -0.5)
    ffn_w_down = normal(ks[19], (DEPTH, FFN_HIDDEN, D), FFN_HIDDEN ** -0.5)

    return {
        "x": x, "mix_norm": mix_norm, "ffn_norm": ffn_norm, "final_norm": final_norm,
        "cv_w_pw1": cv_w_pw1, "cv_b_pw1": cv_b_pw1, "cv_w_dw": cv_w_dw, "cv_b_dw": cv_b_dw,
        "cv_ln_g": cv_ln_g, "cv_ln_b": cv_ln_b, "cv_w_pw2": cv_w_pw2, "cv_b_pw2": cv_b_pw2,
        "dn_w_in": dn_w_in, "dn_w_conv": dn_w_conv, "dn_a_log": dn_a_log, "dn_dt_bias": dn_dt_bias,
        "dn_norm_w": dn_norm_w, "dn_w_out": dn_w_out,
        "ffn_w_gate_up": ffn_w_gate_up, "ffn_w_down": ffn_w_down,
    }


def reference(x, mix_norm, ffn_norm, final_norm,
              cv_w_pw1, cv_b_pw1, cv_w_dw, cv_b_dw, cv_ln_g, cv_ln_b, cv_w_pw2, cv_b_pw2,
              dn_w_in, dn_w_conv, dn_a_log, dn_dt_bias, dn_norm_w, dn_w_out,
              ffn_w_gate_up, ffn_w_down):
    h = x
    for i in range(DEPTH):
        j = i // N_MIXERS
        hn = rmsnorm(h, mix_norm[i])
        if i % N_MIXERS == 0:
            mix = conformer_conv_module(hn, cv_w_pw1[j], cv_b_pw1[j], cv_w_dw[j], cv_b_dw[j],
                                        cv_ln_g[j], cv_ln_b[j], cv_w_pw2[j], cv_b_pw2[j])
        else:
            mix = gated_deltanet_bidir(hn, dn_w_in[j], dn_w_conv[j], dn_a_log[j], dn_dt_bias[j],
                                       dn_norm_w[j], dn_w_out[j])
        h = h + mix.astype(h.dtype)
        h = h + swiglu_ffn(rmsnorm(h, ffn_norm[i]), ffn_w_gate_up[i], ffn_w_down[i]).astype(h.dtype)
    return rmsnorm(h, final_norm)
```

```python
import numpy as np
import concourse.bass as bass
import concourse.mybir as mybir
from concourse.bass_utils import run_bass_kernel_spmd

F32 = mybir.dt.float32
BF16 = mybir.dt.bfloat16
AF = mybir.ActivationFunctionType
ALU = mybir.AluOpType
AX = mybir.AxisListType

ENGS = ("pe", "act", "dve", "pool", "sp")
EPOCH = 24000


class Region:
    __slots__ = ("name", "arena", "lo", "hi", "last_w", "readers", "pending", "parent", "excl")

    def __init__(self, name, arena=None, lo=0, hi=0):
        self.parent = None
        self.excl = False
        self.name = name
        self.arena = arena
        self.lo = lo
        self.hi = hi
        self.last_w = None
        self.readers = []
        self.pending = []


class Op:
    __slots__ = ("eng", "fn", "pos", "gidx", "deps", "is_dma", "dma_sem", "dma_val",
                 "waits", "signal", "signo", "clock", "dclock", "inc")

    def __init__(self, eng, fn, is_dma):
        self.eng = eng
        self.fn = fn
        self.is_dma = is_dma
        self.deps = []
        self.waits = []
        self.signal = False
        self.signo = 0
        self.dma_sem = None
        self.dma_val = 0
        self.inc = 16


class Prog:
    def __init__(self):
        self.ops = {e: [] for e in ENGS}
        self.all_ops = []
        self.arena_regions = {}
        self.dma_groups = {}

    def region(self, name, arena=None, lo=0, hi=0, parent=None):
        r = Region(name, arena, lo, hi)
        if parent is not None:
            r.parent = parent
            r.pending = list(parent.pending) + ([parent.last_w] if parent.last_w is not None else []) \
                + list(parent.readers)
        if arena is not None:
            lst = self.arena_regions.setdefault(arena, [])
            keep = []
            for o in lst:
                if o.lo < hi and lo < o.hi:
                    if o.last_w is not None:
                        r.pending.append(o.last_w)
                    r.pending.extend(o.readers)
                    r.pending.extend(o.pending)
                    if not (lo <= o.lo and o.hi <= hi):
                        keep.append(o)
                else:
                    keep.append(o)
            keep.append(r)
            self.arena_regions[arena] = keep
        return r

    def _add(self, eng, fn, reads, writes, is_dma, sem_group=None):
        op = Op(eng, fn, is_dma)
        deps = {}
        for r in reads:
            if r.last_w is not None:
                deps[id(r.last_w)] = r.last_w
            if r.excl:
                for rd in r.readers:
                    if rd.eng != eng:
                        deps[id(rd)] = rd
            for p in r.pending:
                deps[id(p)] = p
        for w in writes:
            if w.last_w is not None:
                deps[id(w.last_w)] = w.last_w
            for rd in w.readers:
                deps[id(rd)] = rd
            for p in w.pending:
                deps[id(p)] = p
        if sem_group is not None:
            g = self.dma_groups[sem_group]
            slot = g["n"] % len(g["last"])
            prev = g["last"][slot]
            if prev is not None:
                deps[id(prev)] = prev
            g["last"][slot] = op
            g["cnt"][slot] += 1
            op.dma_sem = (sem_group, slot)
            op.dma_val = 16 * g["cnt"][slot]
            g["n"] += 1
        deps.pop(id(op), None)
        op.deps = list(deps.values())
        for r in reads:
            if not is_dma:
                r.readers = [o for o in r.readers if o.is_dma or o.eng != eng]
            r.readers.append(op)
        for w in writes:
            w.last_w = op
            w.readers = []
            w.pending = []
        for x in list(reads) + list(writes):
            pr = x.parent
            if pr is not None:
                if not is_dma:
                    pr.readers = [o for o in pr.readers if o.is_dma or o.eng != eng]
                pr.readers.append(op)
        op.pos = len(self.ops[eng])
        op.gidx = len(self.all_ops)
        self.ops[eng].append(op)
        self.all_ops.append(op)
        return op

    def op(self, eng, fn, reads=(), writes=()):
        return self._add(eng, fn, reads, writes, False)

    def wait_ops(self, eng, deps):
        op = Op(eng, lambda e: None, False)
        op.deps = list(deps)
        op.pos = len(self.ops[eng])
        op.gidx = len(self.all_ops)
        self.ops[eng].append(op)
        self.all_ops.append(op)
        return op

    def dma_group(self, name, nslots):
        self.dma_groups[name] = {"n": 0, "last": [None] * nslots, "cnt": [0] * nslots}

    def dma(self, eng, fn, reads=(), writes=(), group=None, inc=16):
        assert group in self.dma_groups, group
        op = self._add(eng, fn, reads, writes, True, group)
        if inc != 16:
            op.dma_val = (op.dma_val // 16) * inc
        op.inc = inc
        return op

    def plan(self):
        known = {e: {f: -1 for f in ENGS} for e in ENGS}
        dknown = {e: {} for e in ENGS}
        for op in self.all_ops:
            E = op.eng
            kn, dk = known[E], dknown[E]
            for d in sorted(op.deps, key=lambda o: -o.gidx):
                if d.is_dma:
                    if dk.get(d.dma_sem, 0) >= d.dma_val:
                        continue
                    op.waits.append(d)
                else:
                    if d.eng == E and E == "pe":
                        continue
                    if kn[d.eng] >= d.pos:
                        continue
                    op.waits.append(d)
                    d.signal = True
                for f in ENGS:
                    if d.clock[f] > kn[f]:
                        kn[f] = d.clock[f]
                for s, v in d.dclock.items():
                    if dk.get(s, 0) < v:
                        dk[s] = v
            clock = dict(kn)
            dclock = dict(dk)
            if op.is_dma:
                dclock[op.dma_sem] = max(dclock.get(op.dma_sem, 0), op.dma_val)
            else:
                clock[E] = max(clock[E], op.pos)
            op.clock = clock
            op.dclock = dclock
        for e in ENGS:
            n = 0
            for op in self.ops[e]:
                if op.signal and not op.is_dma:
                    n += 1
                    op.signo = n

    def finish(self):
        lasts = [self.ops[e][-1] for e in ENGS if e != "sp" and self.ops[e]]
        self.wait_ops("sp", lasts)

    def emit(self, nc, stack):
        self.finish()
        self.plan()
        nsig = {e: sum(1 for o in self.ops[e] if o.signal and not o.is_dma) for e in ENGS}
        esems = {}
        for e in ENGS:
            k = max(1, (nsig[e] + EPOCH - 1) // EPOCH)
            esems[e] = [stack.enter_context(nc.semaphore(f"q_{e}_{i}")) for i in range(k)]
        dsems = {}
        for gname, g in self.dma_groups.items():
            for slot in range(len(g["last"])):
                dsems[(gname, slot)] = stack.enter_context(nc.semaphore(f"d_{gname}_{slot}"))
        block = stack.enter_context(nc.Block())

        def run(e, eng):
            for op in self.ops[e]:
                for d in op.waits:
                    if d.is_dma:
                        eng.wait_ge(dsems[d.dma_sem], d.dma_val)
                    else:
                        ep = (d.signo - 1) // EPOCH
                        eng.wait_ge(esems[d.eng][ep], d.signo - ep * EPOCH)
                ins = op.fn(eng)
                if ins is None:
                    assert not op.signal and not op.is_dma
                    continue
                if op.is_dma:
                    ins.then_inc(dsems[op.dma_sem], op.inc)
                elif op.signal:
                    ep = (op.signo - 1) // EPOCH
                    ins.then_inc(esems[e][ep], 1)

        @block.tensor
        def _(eng):
            run("pe", eng)

        @block.scalar
        def _(eng):
            run("act", eng)

        @block.vector
        def _(eng):
            run("dve", eng)

        @block.gpsimd
        def _(eng):
            run("pool", eng)

        @block.sync
        def _(eng):
            run("sp", eng)


D = 2048
KC = 16
SEQ = 4096
BATCH = 4
TOK = 2048
EXT = 2
CPAD = 15
NTA = TOK + EXT + CPAD
NTB = TOK + EXT
FFH = 5632
FC = 44
RMS_EPS = 1e-6
LN_EPS = 1e-5

TT_A = [(0, 512), (512, 512), (1024, 512), (1536, 512), (2048, NTA - 2048)]
TT_B = [(0, 512), (512, 512), (1024, 512), (1536, 512), (2048, EXT)]
TT_M = [(0, 512), (512, 512), (1024, 512), (1536, 512)]

PV = {}
_c = 0
for _n, _w in [("mixn0", 16), ("ffnn0", 16), ("mixn1", 16), ("ffnn1", 16), ("finn", 16),
               ("b_pw1", 32), ("b_dw", 16), ("ln_g", 16), ("ln_b", 16), ("b_pw2", 16),
               ("w_dw", 16 * 31), ("w_sc", 64 * 5), ("dnw", 1), ("alog", 1), ("dtb", 1),
               ("msk", 2)]:
    PV[_n] = _c
    _c += _w
NPV = _c


class Ctx:
    def __init__(self, nc, P, stack):
        self.nc, self.P, self.stack = nc, P, stack
        self.ARENA_F = 50176
        self.arena = stack.enter_context(nc.sbuf_tensor("arena", [128, self.ARENA_F], F32))
        self.banks = [stack.enter_context(nc.psum_tensor(f"bank{i}", [128, 512], F32)) for i in range(8)]
        self.bank_reg = [P.region(f"bank{i}") for i in range(8)]
        for r in self.bank_reg:
            r.excl = True

    def sb(self, name, off, nbytes, dtype, shape=None):
        assert off % 4 == 0 and nbytes % 4 == 0, (off, nbytes)
        assert off + nbytes <= self.ARENA_F * 4, (name, off, nbytes)
        ap = self.arena[:, off // 4:(off + nbytes) // 4]
        if dtype != F32:
            ap = ap.bitcast(dtype)
        if shape is not None:
            if len(shape) == 1:
                ap = ap.rearrange("p (a n) -> p a n", a=shape[0])
            elif len(shape) == 2:
                ap = ap.rearrange("p (a b n) -> p a b n", a=shape[0], b=shape[1])
        reg = self.P.region(name, "arena", off, off + nbytes)
        return ap, reg

    def bank(self, i):
        return self.banks[i][:, :], self.bank_reg[i]


def _dma(P, eng, out, in_, reads, writes, group):
    return P.dma(eng, lambda e, o=out, i=in_: e.dma_start(out=o, in_=i), reads, writes, group)


def _mm(P, ps, pairs, reads, writes):
    def fn(e, ps=ps, pairs=pairs):
        n = len(pairs)
        ins = None
        for i, (l, r) in enumerate(pairs):
            ins = e.matmul(ps, lhsT=l, rhs=r, start=(i == 0), stop=(i == n - 1))
        return ins
    return P.op("pe", fn, reads, writes)


def _act(P, out, in_, func, reads, writes, bias=None, scale=None):
    kw = {}
    if bias is not None:
        kw["bias"] = bias
    if scale is not None:
        kw["scale"] = scale
    return P.op("act", lambda e, o=out, i=in_, f=func, kw=kw: e.activation(out=o, in_=i, func=f, **kw), reads, writes)


OFF_CONST = 0
SZ_PV = NPV * 4
OFF_ONES = ((SZ_PV + 63) // 64) * 64
OFF_MISC = OFF_ONES + 256
OFF_ACTA = 8192
ACT_COLS = 2080
SZ_ACTA = KC * ACT_COLS * 2
OFF_WORK = OFF_ACTA + SZ_ACTA


def setup_consts(cx, pv_dram):
    P = cx.P
    cx.pv, cx.pv_r = cx.sb("pv", OFF_CONST, SZ_PV, F32)
    cx.ones, cx.ones_r = cx.sb("ones", OFF_ONES, 256, BF16)
    cx.misc, cx.misc_r = cx.sb("misc", OFF_MISC, 64, F32)
    _dma(P, "sp", cx.pv, pv_dram, [], [cx.pv_r], "ld")
    P.op("pool", lambda e: e.memset(cx.ones, 1.0 / D), [], [cx.ones_r])
    P.op("pool", lambda e: e.memset(cx.misc[:, 0:1], RMS_EPS), [], [cx.misc_r])
    P.op("pool", lambda e: e.memset(cx.misc[:, 1:2], LN_EPS), [cx.misc_r], [cx.misc_r])
    P.op("pool", lambda e: e.memset(cx.misc[:, 2:3], 1.0), [cx.misc_r], [cx.misc_r])
    cx.acta, cx.acta_r = cx.sb("acta", OFF_ACTA, SZ_ACTA, BF16, (KC,))


def pvcol(cx, name, i=0, n=1):
    c = PV[name] + i
    return cx.pv[:, c:c + n]


def stage_rmsnorm(cx, src, tiles, gname, tag, src_regs=lambda t: [], out_dst=None, out_regs=None,
                  work_off=None, nbuf=2):
    P = cx.P
    o = OFF_WORK if work_off is None else work_off
    xt, xt_r = [], []
    for b in range(nbuf):
        a, r = cx.sb(f"{tag}_xt{b}", o, KC * 512 * 4, F32, (KC,))
        xt.append(a); xt_r.append(r); o += KC * 512 * 4
    sq, sq_r = cx.sb(f"{tag}_sq", o, KC * 512 * 2, BF16, (KC,)); o += KC * 512 * 2
    rt, rt_r = cx.sb(f"{tag}_rt", o, 512 * 4, F32); o += 512 * 4
    if out_dst is not None:
        yt, yt_r = cx.sb(f"{tag}_yt", OFF_ACTA, KC * 512 * 4, F32, (KC,))
    ps, ps_r = cx.bank(7)
    if out_dst is None:
        cx.acta, cx.acta_r = cx.sb(f"{tag}_acta", OFF_ACTA, SZ_ACTA, BF16, (KC,))
    acta_t = [P.region(f"{tag}_acta{t}", parent=cx.acta_r) for t in range(len(tiles))]
    cx.acta_tiles = acta_t
    loads = {}

    def load(t):
        s, l = tiles[t]
        loads[t] = _dma(P, "sp", xt[t % nbuf][:, :, :l], src[:, :, s:s + l], src_regs(t), [xt_r[t % nbuf]], "ld")

    if nbuf > 1:
        load(0)
    for t, (s, l) in enumerate(tiles):
        b = t % nbuf
        if nbuf == 1:
            load(t)
        elif t + 1 < len(tiles):
            load(t + 1)
        _act(P, sq[:, :, :l], xt[b][:, :, :l], AF.Square, [xt_r[b]], [sq_r])
        _mm(P, ps[:, :l], [(cx.ones, sq[:, kc, :l]) for kc in range(KC)], [cx.ones_r, sq_r], [ps_r])
        _act(P, rt[:, :l], ps[:, :l], AF.Sqrt, [ps_r, cx.misc_r], [rt_r], bias=cx.misc[:, 0:1], scale=1.0)
        P.op("dve", lambda e, l=l: e.reciprocal(out=rt[:, :l], in_=rt[:, :l]), [rt_r], [rt_r])
        if out_dst is not None:
            for kc in range(KC):
                P.op("dve", lambda e, kc=kc, b=b, l=l: e.scalar_tensor_tensor(
                    out=yt[:, kc, :l], in0=xt[b][:, kc, :l], scalar=pvcol(cx, gname, kc),
                    in1=rt[:, :l], op0=ALU.mult, op1=ALU.mult),
                    [xt_r[b], rt_r, cx.pv_r], [yt_r])
            _dma(P, "sp", out_dst[:, :, s:s + l], yt[:, :, :l], [yt_r], [out_regs[t]], "st")
            continue
        for kc in range(KC):
            P.op("dve", lambda e, kc=kc, b=b, s=s, l=l: e.scalar_tensor_tensor(
                out=cx.acta[:, kc, s:s + l], in0=xt[b][:, kc, :l], scalar=pvcol(cx, gname, kc),
                in1=rt[:, :l], op0=ALU.mult, op1=ALU.mult),
                [xt_r[b], rt_r, cx.pv_r], [acta_t[t]])


def stage_pw1(cx, w_dram, u_scr, u_regs):
    P = cx.P
    o = OFF_WORK
    NW = 3
    wb, wb_r = [], []
    for i in range(NW):
        a, r = cx.sb(f"pw1_w{i}", o, KC * 256 * 2, BF16, (KC,)); wb.append(a); wb_r.append(r); o += KC * 256 * 2
    sg, sg_r, ut, ut_r = [], [], [], []
    for i in range(2):
        a, r = cx.sb(f"pw1_sg{i}", o, 2048, F32); sg.append(a); sg_r.append(r); o += 2048
        a, r = cx.sb(f"pw1_ut{i}", o, 2048, F32); ut.append(a); ut_r.append(r); o += 2048
    zt, zt_r = cx.sb("pw1_z", o, KC * 16 * 4, F32, (KC,)); o += KC * 16 * 4
    P.op("pool", lambda e: e.memset(zt, 0.0), [], [zt_r])
    _dma(P, "sp", u_scr[:, :, 0:CPAD], zt[:, :, 0:CPAD], [zt_r], [u_regs["pad"]], "st")
    tiles = TT_A
    n = 0
    wl = {}

    def wload(j):
        wl[j] = _dma(P, "pool", wb[j % NW], w_dram[j].rearrange("p (k n) -> p k n", k=KC), [], [wb_r[j % NW]], "wld")

    wload(0); wload(1)
    for j in range(KC):
        if j + 2 < KC:
            wload(j + 2)
        w = wb[j % NW]
        for t, (s, l) in enumerate(tiles):
            pa, pa_r = cx.bank((n % 2) * 2)
            pg, pg_r = cx.bank((n % 2) * 2 + 1)
            b = n % 2
            rd = [wb_r[j % NW], cx.acta_tiles[t]]
            _mm(P, pa[:, :l], [(w[:, kc, 0:128], cx.acta[:, kc, s:s + l]) for kc in range(KC)], rd, [pa_r])
            _mm(P, pg[:, :l], [(w[:, kc, 128:256], cx.acta[:, kc, s:s + l]) for kc in range(KC)], rd, [pg_r])
            _act(P, sg[b][:, :l], pg[:, :l], AF.Sigmoid, [pg_r, cx.pv_r], [sg_r[b]],
                 bias=pvcol(cx, "b_pw1", 16 + j), scale=1.0)
            P.op("dve", lambda e, b=b, l=l, pa=pa, j=j: e.scalar_tensor_tensor(
                out=ut[b][:, :l], in0=pa[:, :l], scalar=pvcol(cx, "b_pw1", j), in1=sg[b][:, :l],
                op0=ALU.add, op1=ALU.mult), [pa_r, sg_r[b], cx.pv_r], [ut_r[b]])
            _dma(P, "sp", u_scr[:, j, CPAD + s:CPAD + s + l], ut[b][:, :l], [ut_r[b]], [u_regs[(j, t)]], "st")
            n += 1


def stage_conv_ln(cx, u_scr, u_regs):
    P = cx.P
    TW = 256
    tiles = [(s, TW) for s in range(0, TOK - TW, TW)] + [(TOK - TW, TW + EXT)]
    LMAX = TW + EXT
    UW = LMAX + 2 * CPAD
    o = OFF_WORK
    uh, uh_r = [], []
    for i in range(2):
        a, r = cx.sb(f"cv_uh{i}", o, KC * UW * 4, F32, (KC,)); uh.append(a); uh_r.append(r); o += KC * UW * 4
    co, co_r = cx.sb("cv_co", o, KC * LMAX * 4, F32, (KC,)); o += KC * LMAX * 4
    cb, cb_r = cx.sb("cv_cb", o, KC * LMAX * 2, BF16, (KC,)); o += KC * LMAX * 2
    sqb, sqb_r = cx.sb("cv_sqb", o, KC * LMAX * 2, BF16, (KC,)); o += KC * LMAX * 2
    mean, mean_r = cx.sb("cv_mean", o, LMAX * 4, F32); o += LMAX * 4
    var, var_r = cx.sb("cv_var", o, LMAX * 4, F32); o += LMAX * 4
    NTMP = 8
    tmpc, tmpc_r = [], []
    for i in range(NTMP):
        a, r = cx.sb(f"cv_tmp{i}", o, LMAX * 4, F32); tmpc.append(a); tmpc_r.append(r); o += LMAX * 4
    pm, pm_r = cx.bank(4)
    pq, pq_r = cx.bank(5)
    cx.acta, cx.acta_r = cx.sb("cv_actar", OFF_ACTA, SZ_ACTA, BF16, (KC,))
    acta_t = [P.region(f"cv_acta{t}", parent=cx.acta_r) for t in range(len(tiles))]
    co_k = [P.region(f"cv_co{k}", parent=co_r) for k in range(KC)]
    all_u = list(u_regs.values())

    def load(t):
        s, l = tiles[t]
        _dma(P, "sp", uh[t % 2][:, :, :l + 2 * CPAD], u_scr[:, :, s:s + l + 2 * CPAD], all_u, [uh_r[t % 2]], "ld")

    load(0)
    NCH = 4
    for t, (s, l) in enumerate(tiles):
        b = t % 2
        if t + 1 < len(tiles):
            load(t + 1)
        for k0 in range(0, KC - NCH, NCH):
            for j in range(31):
                for kc in range(k0, k0 + NCH):
                    wcol = cx.pv[:, PV["w_dw"] + kc * 31 + j: PV["w_dw"] + kc * 31 + j + 1]
                    if j == 0:
                        P.op("dve", lambda e, kc=kc, b=b, l=l, wcol=wcol: e.tensor_scalar(
                            out=co[:, kc, :l], in0=uh[b][:, kc, 0:l], scalar1=wcol, scalar2=pvcol(cx, "b_dw", kc),
                            op0=ALU.mult, op1=ALU.add), [uh_r[b], cx.pv_r], [co_k[kc]])
                    else:
                        P.op("dve", lambda e, kc=kc, b=b, l=l, j=j, wcol=wcol: e.scalar_tensor_tensor(
                            out=co[:, kc, :l], in0=uh[b][:, kc, j:j + l], scalar=wcol, in1=co[:, kc, :l],
                            op0=ALU.mult, op1=ALU.add), [uh_r[b], cx.pv_r] if j == 30 else [],
                            [co_k[kc]] if j == 30 else [])
        nsl = 0
        for j in range(31):
            for kc in range(KC - NCH, KC):
                wcol = cx.pv[:, PV["w_dw"] + kc * 31 + j: PV["w_dw"] + kc * 31 + j + 1]
                if j == 0:
                    _act(P, co[:, kc, :l], uh[b][:, kc, 0:l], AF.Identity, [uh_r[b], cx.pv_r], [co_k[kc]],
                         bias=pvcol(cx, "b_dw", kc), scale=wcol)
                else:
                    sl = nsl % NTMP
                    nsl += 1
                    _act(P, tmpc[sl][:, :l], uh[b][:, kc, j:j + l], AF.Identity, [uh_r[b], cx.pv_r], [tmpc_r[sl]],
                         scale=wcol)
                    P.op("pool", lambda e, kc=kc, l=l, sl=sl: e.tensor_tensor(out=co[:, kc, :l], in0=co[:, kc, :l],
                                                                              in1=tmpc[sl][:, :l], op=ALU.add),
                         [tmpc_r[sl], co_k[kc]], [co_k[kc]])
        _act(P, cb[:, :, :l], co[:, :, :l], AF.Copy, co_k, [cb_r])
        _act(P, sqb[:, :, :l], co[:, :, :l], AF.Square, co_k, [sqb_r])
        _mm(P, pm[:, :l], [(cx.ones, cb[:, kc, :l]) for kc in range(KC)], [cx.ones_r, cb_r], [pm_r])
        _mm(P, pq[:, :l], [(cx.ones, sqb[:, kc, :l]) for kc in range(KC)], [cx.ones_r, sqb_r], [pq_r])
        _act(P, mean[:, :l], pm[:, :l], AF.Copy, [pm_r], [mean_r])
        P.op("dve", lambda e, l=l: e.tensor_tensor(out=var[:, :l], in0=mean[:, :l], in1=mean[:, :l], op=ALU.mult),
             [mean_r], [var_r])
        P.op("dve", lambda e, l=l: e.tensor_tensor(out=var[:, :l], in0=pq[:, :l], in1=var[:, :l], op=ALU.subtract),
             [pq_r, var_r], [var_r])
        _act(P, var[:, :l], var[:, :l], AF.Sqrt, [var_r, cx.misc_r], [var_r], bias=cx.misc[:, 1:2], scale=1.0)
        P.op("dve", lambda e, l=l: e.reciprocal(out=var[:, :l], in_=var[:, :l]), [var_r], [var_r])
        for kc in range(KC):
            P.op("pool", lambda e, kc=kc, l=l: e.tensor_tensor(out=co[:, kc, :l], in0=co[:, kc, :l], in1=mean[:, :l],
                                                               op=ALU.subtract), [co_k[kc], mean_r], [co_k[kc]])
            P.op("dve", lambda e, kc=kc, l=l: e.tensor_tensor(out=co[:, kc, :l], in0=co[:, kc, :l], in1=var[:, :l],
                                                              op=ALU.mult), [co_k[kc], var_r], [co_k[kc]])
            _act(P, cx.acta[:, kc, s:s + l], co[:, kc, :l], AF.Silu, [co_k[kc], cx.pv_r], [acta_t[t]],
                 bias=pvcol(cx, "ln_b", kc), scale=pvcol(cx, "ln_g", kc))
    cx.acta_tiles_fine = (tiles, acta_t)


def stage_proj_res(cx, w_dram, nblk, bw, kchunks, tiles, act_reads, bias_name, res_src, res_regs, dst, dst_regs,
                   tag, act_ap=None):
    P = cx.P
    o = cx.work_off
    NW = 3
    wb, wb_r = [], []
    for i in range(NW):
        a, r = cx.sb(f"{tag}_w{i}", o, kchunks * bw * 2, BF16, (kchunks,)); wb.append(a); wb_r.append(r)
        o += kchunks * bw * 2
    rs, rs_r, ot, ot_r = [], [], [], []
    for i in range(3):
        a, r = cx.sb(f"{tag}_rs{i}", o, 2048, F32); rs.append(a); rs_r.append(r); o += 2048
        a, r = cx.sb(f"{tag}_ot{i}", o, 2048, F32); ot.append(a); ot_r.append(r); o += 2048
    act = cx.acta if act_ap is None else act_ap

    def wload(j):
        _dma(P, "pool", wb[j % NW], w_dram[j].rearrange("p (k n) -> p k n", k=kchunks), [], [wb_r[j % NW]], "wld")

    wload(0)
    if nblk > 1:
        wload(1)
    n = 0
    for j in range(nblk):
        if j + 2 < nblk:
            wload(j + 2)
        w = wb[j % NW]
        for t, (s, l, a0) in enumerate(tiles):
            b = n % 3
            ps, ps_r = cx.bank(n % 3)
            _dma(P, "sp", rs[b][:, :l], res_src[:, j, s:s + l], [res_regs[(j, t)]], [rs_r[b]], "ld")
            _mm(P, ps[:, :l], [(w[:, kc, :], act[:, kc, a0:a0 + l]) for kc in range(kchunks)],
                [wb_r[j % NW]] + act_reads(t), [ps_r])
            if bias_name is not None:
                P.op("dve", lambda e, b=b, l=l, ps=ps, j=j: e.scalar_tensor_tensor(
                    out=ot[b][:, :l], in0=ps[:, :l], scalar=pvcol(cx, bias_name, j), in1=rs[b][:, :l],
                    op0=ALU.add, op1=ALU.add), [ps_r, rs_r[b], cx.pv_r], [ot_r[b]])
            else:
                P.op("dve", lambda e, b=b, l=l, ps=ps: e.tensor_tensor(
                    out=ot[b][:, :l], in0=ps[:, :l], in1=rs[b][:, :l], op=ALU.add), [ps_r, rs_r[b]], [ot_r[b]])
            _dma(P, "sp", dst[:, j, s:s + l], ot[b][:, :l], [ot_r[b]], [dst_regs[(j, t)]], "st")
            n += 1


def stage_gate_up(cx, w_dram, hid_scr, hid_regs, tiles):
    P = cx.P
    o = OFF_WORK
    NW = 3
    wb, wb_r = [], []
    for i in range(NW):
        a, r = cx.sb(f"gu_w{i}", o, KC * 256 * 2, BF16, (KC,)); wb.append(a); wb_r.append(r); o += KC * 256 * 2
    sg, sg_r, ht, ht_r = [], [], [], []
    for i in range(2):
        a, r = cx.sb(f"gu_sg{i}", o, 2048, F32); sg.append(a); sg_r.append(r); o += 2048
        a, r = cx.sb(f"gu_ht{i}", o, 1024, BF16); ht.append(a); ht_r.append(r); o += 1024

    def wload(j):
        _dma(P, "pool", wb[j % NW], w_dram[j].rearrange("p (k n) -> p k n", k=KC), [], [wb_r[j % NW]], "wld")

    wload(0); wload(1)
    n = 0
    for j in range(FC):
        if j + 2 < FC:
            wload(j + 2)
        w = wb[j % NW]
        for t, (s, l) in enumerate(tiles):
            b = n % 2
            pg, pg_r = cx.bank((n % 2) * 2)
            pu, pu_r = cx.bank((n % 2) * 2 + 1)
            rd = [wb_r[j % NW], cx.acta_tiles[t]]
            _mm(P, pg[:, :l], [(w[:, kc, 0:128], cx.acta[:, kc, s:s + l]) for kc in range(KC)], rd, [pg_r])
            _mm(P, pu[:, :l], [(w[:, kc, 128:256], cx.acta[:, kc, s:s + l]) for kc in range(KC)], rd, [pu_r])
            _act(P, sg[b][:, :l], pg[:, :l], AF.Silu, [pg_r], [sg_r[b]])
            P.op("dve", lambda e, b=b, l=l, pu=pu: e.tensor_tensor(out=ht[b][:, :l], in0=pu[:, :l], in1=sg[b][:, :l],
                                                                   op=ALU.mult), [pu_r, sg_r[b]], [ht_r[b]])
            _dma(P, "sp", hid_scr[:, j, s:s + l], ht[b][:, :l], [ht_r[b]], [hid_regs[(j, t)]], "st")
            n += 1


def stage_down(cx, w_dram, hid_scr, hid_regs, h_scr, h_regs, tiles, FC=FC, tag="dn"):
    P = cx.P
    passes = [[0, 1], [2, 3] + ([4] if len(tiles) > 4 else [])]
    for pi, tl in enumerate(passes):
        s0 = tiles[tl[0]][0]
        ncol = sum(tiles[t][1] for t in tl)
        hid, hid_r = cx.sb(f"{tag}_hid{pi}", OFF_ACTA, FC * 1028 * 2, BF16, (FC,))
        hr = [P.region(f"{tag}_hid{pi}_{t}", parent=hid_r) for t in tl]
        GS = 11 if FC % 11 == 0 else 8
        for ti, t in enumerate(tl):
            s, l = tiles[t]
            for g in range(0, FC, GS):
                _dma(P, "sp", hid[:, g:g + GS, s - s0:s - s0 + l], hid_scr[:, g:g + GS, s:s + l],
                     [hid_regs[(j, t)] for j in range(g, g + GS)], [hr[ti]], "ld")
        cx.work_off = OFF_ACTA + FC * 1028 * 2
        stage_proj_res(cx, w_dram, KC, 128, FC, [(tiles[t][0], tiles[t][1], tiles[t][0] - s0) for t in tl],
                       lambda t, hr=hr: [hr[t]], None, h_scr,
                       {(j, ti): h_regs[(j, t)] for j in range(KC) for ti, t in enumerate(tl)}, h_scr,
                       {(j, ti): h_regs[(j, t)] for j in range(KC) for ti, t in enumerate(tl)}, f"{tag}{pi}", act_ap=hid)


def dview(t, k):
    return t.ap().rearrange("(k p) n -> p k n", p=128)


def build_program(mode="full", ncores=8):
    from contextlib import ExitStack
    nc = bass.Bass("TRN2", target_bir_lowering=False)
    P = Prog()
    P.dma_group("ld", 6)
    P.dma_group("st", 6)
    P.dma_group("wld", 3)
    P.dma_group("cc", 1)
    xT = nc.dram_tensor("xT", [D, NTA], F32, kind="ExternalInput")
    pv = nc.dram_tensor("pv", [128, NPV], F32, kind="ExternalInput")
    w_pw1 = nc.dram_tensor("w_pw1", [KC, 128, KC * 256], F32, kind="ExternalInput")
    w_pw2 = nc.dram_tensor("w_pw2", [KC, 128, KC * 128], F32, kind="ExternalInput")
    w_gu0 = nc.dram_tensor("w_gu0", [FC, 128, KC * 256], F32, kind="ExternalInput")
    w_dn0 = nc.dram_tensor("w_dn0", [KC, 128, FC * 128], F32, kind="ExternalInput")
    u_scr = nc.dram_tensor("u_scr", [D, CPAD + NTA], F32, kind="ExternalOutput" if mode == "l0a" else "Internal")
    hid_scr = nc.dram_tensor("hid_scr", [FFH, NTB], BF16, kind="Internal")
    if mode in ("l0", "l0a"):
        h_scr = nc.dram_tensor("hout", [D, NTB], F32, kind="ExternalOutput")
    else:
        h_scr = nc.dram_tensor("h_scr", [D, NTB], F32, kind="Internal")
    xv, uv, hv, hidv = dview(xT, KC), dview(u_scr, KC), dview(h_scr, KC), dview(hid_scr, FC)
    full = mode not in ("l0", "l0a", "l0b")
    if full:
        cmat = nc.dram_tensor("cmat", [128, NCM], F32, kind="ExternalInput")
        lvl = {"dbg_l0i": -1, "dbg_l1s": -1, "dbg_n2": 0, "dbg_g": 1, "dbg_ip": 2, "dbg_P": 3, "dbg_X": 3, "dbg_S": 3}.get(mode, 9)
        w_gate = nc.dram_tensor("w_gate", [128, KC * 128], F32, kind="ExternalInput") if lvl >= 1 else None
        w_in = nc.dram_tensor("w_in", [NPROJ, 128, KC * 128], F32, kind="ExternalInput") if lvl >= 2 else None
        if lvl >= 9:
            w_out = nc.dram_tensor("w_out", [KC, 128, 32 * 128], F32, kind="ExternalInput")
            w_gu1 = nc.dram_tensor("w_gu1", [FC, 128, KC * 256], F32, kind="ExternalInput")
            w_dn1 = nc.dram_tensor("w_dn1", [KC, 128, FC * 128], F32, kind="ExternalInput")
        dbg = mode.startswith("dbg")
        DK = "ExternalOutput" if (dbg and lvl >= 3) else "Internal"
        praw_t = nc.dram_tensor("praw", [NPROJ * 128, NTB], F32, kind="ExternalOutput" if (dbg and lvl >= 2) else "Internal")
        qk_t = nc.dram_tensor("qk_scr", [NKH, 128, 2 * TOK], BF16, kind=DK)
        ktm_t = nc.dram_tensor("ktm_scr", [NKH, 128, NCK * 128], BF16, kind=DK)
        vtm_t = nc.dram_tensor("vtm_scr", [NKH, 128, NCK * 256], BF16, kind=DK)
        o_t = nc.dram_tensor("o_scr", [NKH, 128, 2 * TOK], F32, kind=DK)
        st_t = nc.dram_tensor("st_scr", [NKH * 128, 256], F32, kind="Internal")
        stall_t = nc.dram_tensor("st_all", [2 * NKH * 128, 256], F32, kind="Internal")
        og_t = nc.dram_tensor("og_scr", [VD, TOK], BF16, kind=DK)
        if not dbg:
            out_t = nc.dram_tensor("out", [D, TOK], F32, kind="ExternalOutput")

    with ExitStack() as stack:
        cx = Ctx(nc, P, stack)
        setup_consts(cx, pv.ap())
        u_regs = {(j, t): P.region(f"u{j}_{t}") for j in range(KC) for t in range(len(TT_A))}
        u_regs["pad"] = P.region("upad")
        h_regs = {(j, t): P.region(f"h{j}_{t}") for j in range(KC) for t in range(len(TT_B))}
        hid_regs = {(j, t): P.region(f"hid{j}_{t}") for j in range(FC) for t in range(len(TT_B))}
        x_regs = {(j, t): P.region(f"x{j}_{t}") for j in range(KC) for t in range(len(TT_B))}

        stage_rmsnorm(cx, xv, TT_A, "mixn0", "n0")
        stage_pw1(cx, w_pw1.ap(), uv, u_regs)
        stage_conv_ln(cx, uv, u_regs)
        fine = cx.acta_tiles_fine[1]
        if mode == "l0b":
            vdbg = nc.dram_tensor("vdbg", [D, ACT_COLS], BF16, kind="ExternalOutput")
            dd = _dma(P, "sp", dview(vdbg, KC), cx.acta, fine, [P.region("vdbg")], "st")
            P.wait_ops("sp", [dd])
            P.emit(nc, stack)
            return nc
        cx.work_off = OFF_WORK
        stage_proj_res(cx, w_pw2.ap(), KC, 128, KC, [(s, l, s) for (s, l) in TT_B], lambda t: fine, "b_pw2",
                       xv, x_regs, hv, h_regs, "pw2")
        if mode != "l0a":
            stage_rmsnorm(cx, hv, TT_B, "ffnn0", "n1", src_regs=lambda t: [h_regs[(j, t)] for j in range(KC)])
            stage_gate_up(cx, w_gu0.ap(), hidv, hid_regs, TT_B)
            stage_down(cx, w_dn0.ap(), hidv, hid_regs, hv, h_regs, TT_B)
        if not full:
            P.wait_ops("sp", [r.last_w for r in h_regs.values()])
            P.emit(nc, stack)
            return nc

        if mode == "dbg_l0i":
            hd = nc.dram_tensor("hdump", [D, NTB], F32, kind="ExternalOutput")
            dd = _dma(P, "sp", hd.ap(), h_scr.ap(), list(h_regs.values()), [P.region("hd")], "st")
            P.wait_ops("sp", [dd])
            P.emit(nc, stack)
            return nc
        l1_setup(cx, cmat.ap())
        if mode == "dbg_l1s":
            hd = nc.dram_tensor("hdump", [D, NTB], F32, kind="ExternalOutput")
            dd = _dma(P, "sp", hd.ap(), h_scr.ap(), list(h_regs.values()), [P.region("hd")], "st")
            P.wait_ops("sp", [dd, cx.L.cb_r.last_w, cx.L.nA_r.last_w])
            P.emit(nc, stack)
            return nc
        stage_rmsnorm(cx, hv, TT_B, "mixn1", "n2", src_regs=lambda t: [h_regs[(j, t)] for j in range(KC)],
                      work_off=cx.L.off_end, nbuf=2)
        if mode == "dbg_n2":
            vdbg = nc.dram_tensor("vdbg", [D, ACT_COLS], BF16, kind="ExternalOutput")
            dd = _dma(P, "sp", dview(vdbg, KC), cx.acta, cx.acta_tiles, [P.region("vdbg")], "st")
            P.wait_ops("sp", [dd, cx.L.cb_r.last_w, cx.L.nA_r.last_w])
            P.emit(nc, stack)
            return nc
        stage_gates(cx, w_gate.ap())
        if mode == "dbg_g":
            gd = nc.dram_tensor("gdump", [5, 128, NCK * 64], F32, kind="ExternalOutput")
            L = cx.L
            dd = []
            for i, (a, r) in enumerate([(L.gam, L.gam_r), (L.beta, L.beta_r), (L.bg, L.bg_r), (L.ekg, L.ekg_r), (L.dl, L.dl_r)]):
                dd.append(_dma(P, "sp", gd.ap()[i], a.rearrange("p a n -> p (a n)"), [r], [P.region(f"gd{i}")], "st"))
            P.wait_ops("sp", dd)
            P.emit(nc, stack)
            return nc
        praw = dview(praw_t, NPROJ)
        praw_regs = [P.region(f"praw{j}") for j in range(NPROJ)]
        stage_inproj(cx, w_in.ap(), praw, praw_regs)
        if mode == "dbg_ip":
            gd = nc.dram_tensor("gdump", [5, 128, NCK * 64], F32, kind="ExternalOutput")
            L = cx.L
            dd = []
            for i, (a, r) in enumerate([(L.gam, L.gam_r), (L.beta, L.beta_r), (L.bg, L.bg_r), (L.ekg, L.ekg_r), (L.dl, L.dl_r)]):
                dd.append(_dma(P, "sp", gd.ap()[i], a.rearrange("p a n -> p (a n)"), [r], [P.region(f"gd{i}")], "st"))
            P.wait_ops("sp", dd + [r.last_w for r in praw_regs])
            P.emit(nc, stack)
            return nc
        scr = {
            "qk": [qk_t.ap()[k] for k in range(NKH)], "qk_r": [P.region(f"qks{k}") for k in range(NKH)],
            "ktm": [ktm_t.ap()[k] for k in range(NKH)], "ktm_r": [P.region(f"ktms{k}") for k in range(NKH)],
            "vtm": [vtm_t.ap()[k] for k in range(NKH)], "vtm_r": [P.region(f"vtms{k}") for k in range(NKH)],
            "o": [o_t.ap()[k].rearrange("p (u n) -> p u n", u=2) for k in range(NKH)],
            "o_r": [P.region(f"os{k}") for k in range(NKH)],
            "st": [st_t.ap().rearrange("(k p) n -> k p n", p=128)[k] for k in range(NKH)], "st_r": P.region("sts"),
            "stall": stall_t.ap().rearrange("(s k p) n -> p s k n", s=2, p=128), "stall_r": P.region("stall"),
        }
        J = l1_alloc_job(cx, OFF_ACTA)
        assert J.off_end <= OFF_L1C, J.off_end
        G = l1_alloc_group(cx, cx.L.off_end, "P")
        nkh = 2 if mode in ("dbg_P", "dbg_X", "dbg_S") else NKH
        phase_P(cx, J, G, praw, praw_regs, scr, nkh=nkh)
        if mode == "dbg_P":
            std = nc.dram_tensor("st_dump", [128, 256], F32, kind="ExternalOutput")
            d1 = _dma(P, "sp", std.ap(), J.S.rearrange("p u n -> p (u n)"), [J.S_r], [P.region("std")], "st")
            P.wait_ops("sp", [d1, scr["o_r"][0].last_w, scr["qk_r"][0].last_w, scr["ktm_r"][0].last_w, scr["vtm_r"][0].last_w])
            P.emit(nc, stack)
            return nc
        P.dma("pool", lambda e: e.collective_compute(
            "AllGather", ALU.bypass, replica_groups=[[2 * i, 2 * i + 1] for i in range(ncores // 2)],
            ins=[st_t.ap()], outs=[stall_t.ap()]), [scr["st_r"]], [scr["stall_r"]], "cc", inc=1)
        G2 = l1_alloc_group(cx, cx.L.off_end, "S")
        ogv = dview(og_t, 32)
        og_regs = [P.region(f"og{h}") for h in range(32)]
        if mode == "dbg_X":
            sd = nc.dram_tensor("stall_dump", [2 * NKH * 128, 256], F32, kind="ExternalOutput")
            dd = _dma(P, "sp", sd.ap(), stall_t.ap(), [scr["stall_r"]], [P.region("sd")], "st")
            P.wait_ops("sp", [dd])
            P.emit(nc, stack)
            return nc
        phase_S(cx, J, G2, praw, praw_regs, scr, ogv, og_regs, nkh=nkh)
        if mode == "dbg_S":
            P.wait_ops("sp", [og_regs[i].last_w for i in range(4)])
            P.emit(nc, stack)
            return nc
        if mode == "dbg_og":
            P.wait_ops("sp", [r.last_w for r in og_regs])
        hm_regs = {(j, t): h_regs[(j, t)] for j in range(KC) for t in range(4)}
        stage_down(cx, w_out.ap(), ogv, {(j, t): og_regs[j] for j in range(32) for t in range(4)}, hv, hm_regs, TT_M,
                   FC=32, tag="op")
        stage_rmsnorm(cx, hv, TT_M, "ffnn1", "n3", src_regs=lambda t: [h_regs[(j, t)] for j in range(KC)])
        stage_gate_up(cx, w_gu1.ap(), hidv, hid_regs, TT_M)
        stage_down(cx, w_dn1.ap(), hidv, hid_regs, hv, hm_regs, TT_M, tag="dn1_")
        out_regs = [P.region(f"out{t}") for t in range(4)]
        stage_rmsnorm(cx, hv, TT_M, "finn", "n4", src_regs=lambda t: [h_regs[(j, t)] for j in range(KC)],
                      out_dst=dview(out_t, KC), out_regs=out_regs)
        P.wait_ops("sp", [r.last_w for r in out_regs])
        P.emit(nc, stack)
    return nc


def _blk(w, bw):
    K, N = w.shape
    kc = K // 128
    return np.ascontiguousarray(w.reshape(kc, 128, N // bw, bw).transpose(2, 1, 0, 3)).reshape(N // bw, 128, kc * bw)


def _blk_pair(w, half, bw=128):
    K = w.shape[0]
    kc = K // 128
    a = w[:, :half].reshape(kc, 128, half // bw, bw)
    b = w[:, half:].reshape(kc, 128, half // bw, bw)
    ab = np.concatenate([a, b], axis=3)
    return np.ascontiguousarray(ab.transpose(2, 1, 0, 3)).reshape(half // bw, 128, kc * 2 * bw)


def _cols(v):
    return np.ascontiguousarray(v.reshape(-1, 128).T)


def host_prep(inp):
    f = np.float32
    shared = {}
    shared["w_pw1"] = _blk_pair(np.asarray(inp["cv_w_pw1"][0], f), D)
    shared["w_pw2"] = _blk(np.asarray(inp["cv_w_pw2"][0], f), 128)
    shared["w_gu0"] = _blk_pair(np.asarray(inp["ffn_w_gate_up"][0], f), FFH)
    shared["w_dn0"] = _blk(np.asarray(inp["ffn_w_down"][0], f), 128)
    shared["w_in"] = _blk(np.asarray(inp["dn_w_in"][0][:, :NPROJ * 128], f), 128)
    shared["w_out"] = _blk(np.asarray(inp["dn_w_out"][0], f), 128)
    shared["w_gu1"] = _blk_pair(np.asarray(inp["ffn_w_gate_up"][1], f), FFH)
    shared["w_dn1"] = _blk(np.asarray(inp["ffn_w_down"][1], f), 128)
    shared["cmat"] = host_cmat()
    wg = np.asarray(inp["dn_w_in"][0][:, NPROJ * 128:], f)
    wgs = [wg, np.concatenate([wg[:, 32:64], wg[:, 0:32], wg[:, 96:128], wg[:, 64:96]], axis=1)]
    wgs = [_blk(w, 128)[0] for w in wgs]
    pvs = []
    for par in range(2):
        pv = np.zeros((128, NPV), f)

        def put(name, arr):
            pv[:, PV[name]:PV[name] + arr.shape[1]] = arr
        put("mixn0", _cols(inp["mix_norm"][0])); put("ffnn0", _cols(inp["ffn_norm"][0]))
        put("mixn1", _cols(inp["mix_norm"][1])); put("ffnn1", _cols(inp["ffn_norm"][1]))
        put("finn", _cols(inp["final_norm"]))
        put("b_pw1", _cols(inp["cv_b_pw1"][0])); put("b_dw", _cols(inp["cv_b_dw"][0]))
        put("ln_g", _cols(inp["cv_ln_g"][0])); put("ln_b", _cols(inp["cv_ln_b"][0]))
        put("b_pw2", _cols(inp["cv_b_pw2"][0]))
        wdw = np.asarray(inp["cv_w_dw"][0], f)
        if par:
            wdw = wdw[::-1]
        put("w_dw", np.ascontiguousarray(wdw.T.reshape(KC, 128, 31).transpose(1, 0, 2)).reshape(128, KC * 31))
        wsc = np.asarray(inp["dn_w_conv"][0], f)
        if par:
            wsc = wsc[::-1]
        put("w_sc", np.ascontiguousarray(wsc.T.reshape(64, 128, 5).transpose(1, 0, 2)).reshape(128, 64 * 5))
        put("dnw", np.asarray(inp["dn_norm_w"][0], f).reshape(128, 1))
        al = np.asarray(inp["dn_a_log"][0], f)
        dtb = np.asarray(inp["dn_dt_bias"][0], f)
        order = [1, 0] if par else [0, 1]
        pv[64:128, PV["alog"]] = np.concatenate([al[order[0]], al[order[1]]])
        pv[64:128, PV["dtb"]] = np.concatenate([dtb[order[0]], dtb[order[1]]])
        pv[:, PV["msk"]] = 1.0 if par else 0.0
        pv[:, PV["msk"] + 1] = 0.0 if par else 1.0
        pvs.append(pv)
    x = np.asarray(inp["x"], f)
    in_maps = []
    for c in range(8):
        b, par = c // 2, c % 2
        xs = x[b]
        if par:
            xs = xs[::-1]
        m = dict(shared)
        m["xT"] = np.ascontiguousarray(xs[:NTA].T)
        m["pv"] = pvs[par]
        m["w_gate"] = wgs[par]
        in_maps.append(m)
    return in_maps


_NC_CACHE = {}


def kernel(**inputs):
    if "full" not in _NC_CACHE:
        _NC_CACHE["full"] = build_program("full")
    nc = _NC_CACHE["full"]
    in_maps = host_prep(inputs)
    names = {"xT", "pv", "w_pw1", "w_pw2", "w_gu0", "w_dn0", "cmat", "w_in", "w_gate", "w_out", "w_gu1", "w_dn1"}
    in_maps = [{k: v for k, v in m.items() if k in names} for m in in_maps]
    res = run_bass_kernel_spmd(nc, in_maps, core_ids=list(range(8)))
    out = np.empty((BATCH, SEQ, D), np.float32)
    for c in range(8):
        b, par = c // 2, c % 2
        y = res.results[c]["out"].T
        if par:
            out[b, SEQ - TOK:] = y[::-1]
        else:
            out[b, :TOK] = y
    return out


NH = 32
NKH = 16
CH = 128
NCK = TOK // CH
VD = 4096
CM = {"ident": 0, "trif": 128, "trir": 256, "negf": 384, "negr": 512, "nstf": 640, "nstr": 768}
NCM = 896
OFF_L1C = 8192 + SZ_ACTA


def host_cmat():
    j = np.arange(128)[:, None]
    i = np.arange(128)[None, :]
    m = np.zeros((128, NCM), np.float32)
    m[:, 0:128] = (i == j)
    m[:, 128:256] = (j <= i)
    m[:, 256:384] = (j >= i)
    m[:, 384:512] = np.where(i >= j, 0.0, -30000.0)
    m[:, 512:640] = np.where(i <= j, 0.0, -30000.0)
    m[:, 640:768] = np.where(i > j, -1.0, 0.0)
    m[:, 768:896] = np.where(i < j, -1.0, 0.0)
    return m


def _bc(ap, n):
    return ap.to_broadcast([128, n])


class L1:
    pass


def l1_setup(cx, cmat_dram):
    P = cx.P
    L = L1()
    cx.L = L
    o = OFF_L1C
    L.cm, L.cm_r = cx.sb("cm", o, NCM * 4, F32); o += NCM * 4
    L.cb, L.cb_r = cx.sb("cmb", o, 3 * 128 * 2, BF16); o += 3 * 128 * 2
    L.ones1, L.ones1_r = cx.sb("ones1", o, 256, BF16); o += 256
    L.ones128, L.ones128_r = cx.sb("ones128", o, 256, BF16); o += 256
    L.nA, L.nA_r = cx.sb("nA", o, 64, F32); o += 64
    _dma(P, "sp", L.cm, cmat_dram, [], [L.cm_r], "ld")
    P.op("dve", lambda e: e.tensor_copy(out=L.cb, in_=L.cm[:, 0:384]), [L.cm_r], [L.cb_r])
    P.op("pool", lambda e: e.memset(L.ones1, 1.0), [], [L.ones1_r])
    P.op("pool", lambda e: e.memset(L.ones128, 1.0 / 128), [], [L.ones128_r])
    _act(P, L.nA[:, 0:1], pvcol(cx, "alog"), AF.Exp, [cx.pv_r], [L.nA_r])
    P.op("dve", lambda e: e.tensor_scalar(out=L.nA[:, 0:1], in0=L.nA[:, 0:1], scalar1=-1.0, scalar2=None,
                                          op0=ALU.mult), [L.nA_r], [L.nA_r])
    L.identb = L.cb[:, 0:128]
    L.trib = [L.cb[:, 128:256], L.cb[:, 256:384]]
    L.identf = L.cm[:, 0:128]
    L.neg = [L.cm[:, CM["negf"]:CM["negf"] + 128], L.cm[:, CM["negr"]:CM["negr"] + 128]]
    L.nst = [L.cm[:, CM["nstf"]:CM["nstf"] + 128], L.cm[:, CM["nstr"]:CM["nstr"] + 128]]
    def g(name, dt, sz):
        nonlocal o
        a, r = cx.sb(name, o, NCK * 64 * sz, dt, (NCK,)); o += NCK * 64 * sz
        return a, r
    L.beta, L.beta_r = g("g_beta", F32, 4)
    L.gam, L.gam_r = g("g_gam", F32, 4)
    L.bg, L.bg_r = g("g_bg", F32, 4)
    L.ekg, L.ekg_r = g("g_ekg", F32, 4)
    L.dl, L.dl_r = g("g_dl", F32, 4)
    L.ghi, L.ghi_r = g("g_hi", BF16, 2)
    L.glo, L.glo_r = g("g_lo", BF16, 2)
    L.bb, L.bb_r = g("g_bb", BF16, 2)
    L.off_end = o


def stage_gates(cx, wg_dram):
    P, L = cx.P, cx.L
    o = L.off_end
    wb, wb_r = cx.sb("gt_w", o, KC * 128 * 2, BF16, (KC,)); o += KC * 128 * 2
    gb, gb_r = cx.sb("gt_gb", o, TOK * 4, F32); o += TOK * 4
    gc, gc_r = cx.sb("gt_gc", o, NCK * 128 * 4, F32, (NCK,)); o += NCK * 128 * 4
    tmp, tmp_r = cx.sb("gt_tmp", o, NCK * 64 * 4, F32, (NCK,)); o += NCK * 64 * 4
    _dma(P, "pool", wb, wg_dram.rearrange("p (k n) -> p k n", k=KC), [], [wb_r], "wld")
    gb_t = [P.region(f"gt_gb{t}", parent=gb_r) for t in range(4)]
    for t, (s, l) in enumerate(TT_M):
        ps, ps_r = cx.bank(t % 2)
        _mm(P, ps[:, :l], [(wb[:, kc, :], cx.acta[:, kc, s:s + l]) for kc in range(KC)], [wb_r, cx.acta_tiles[t]], [ps_r])
        _act(P, gb[0:64, s:s + l], ps[0:64, :l], AF.Sigmoid, [ps_r], [gb_t[t]])
        _act(P, gb[64:128, s:s + l], ps[64:128, :l], AF.Exp, [ps_r, cx.pv_r], [gb_t[t]], bias=cx.pv[64:128, PV["dtb"]:PV["dtb"] + 1], scale=1.0)
    for t, (s, l) in enumerate(TT_M):
        _act(P, gb[64:128, s:s + l], gb[64:128, s:s + l], AF.Ln, [gb_t[t], cx.misc_r], [gb_t[t]], bias=cx.misc[64:128, 2:3], scale=1.0)
        P.op("dve", lambda e, s=s, l=l: e.tensor_scalar(out=gb[64:128, s:s + l], in0=gb[64:128, s:s + l],
                                                        scalar1=L.nA[64:128, 0:1], scalar2=None, op0=ALU.mult),
             [gb_t[t], L.nA_r], [gb_t[t]])
    for c in range(NCK):
        ps, ps_r = cx.bank(2 + c % 2)
        P.op("pe", lambda e, c=c, ps=ps: e.transpose(ps[:, 0:128], gb[:, c * 128:(c + 1) * 128], L.identf),
             [gb_t[c // 4], L.cm_r], [ps_r])
        if c % 2 == 0:
            _act(P, gc[:, c, :], ps[:, 0:128], AF.Copy, [ps_r], [gc_r])
        else:
            P.op("dve", lambda e, c=c, ps=ps: e.tensor_copy(out=gc[:, c, :], in_=ps[:, 0:128]), [ps_r], [gc_r])
    P.op("dve", lambda e: e.tensor_copy(out=L.beta, in_=gc[:, :, 0:64]), [gc_r], [L.beta_r])
    P.op("dve", lambda e: e.tensor_copy(out=L.bb, in_=gc[:, :, 0:64]), [gc_r], [L.bb_r])
    P.op("dve", lambda e: e.tensor_copy(out=L.ghi, in_=gc[:, :, 64:128]), [gc_r], [L.ghi_r])
    P.op("dve", lambda e: e.tensor_copy(out=tmp, in_=L.ghi), [L.ghi_r], [tmp_r])
    P.op("dve", lambda e: e.tensor_tensor(out=tmp, in0=gc[:, :, 64:128], in1=tmp, op=ALU.subtract), [gc_r, tmp_r], [tmp_r])
    P.op("dve", lambda e: e.tensor_copy(out=L.glo, in_=tmp), [tmp_r], [L.glo_r])
    for c in range(NCK):
        ps, ps_r = cx.bank(4 + c % 2)
        def fn(e, c=c, ps=ps):
            ins = None
            for d in range(2):
                for k, src in enumerate((L.ghi, L.glo)):
                    ins = e.matmul(ps[:, d * 32:(d + 1) * 32], lhsT=L.trib[d], rhs=src[:, c, d * 32:(d + 1) * 32],
                                   start=(k == 0), stop=(k == 1))
            for k, src in enumerate((L.ghi, L.glo)):
                ins = e.matmul(ps[:, 64:128], lhsT=L.ones1, rhs=src[:, c, :], start=(k == 0), stop=(k == 1))
            return ins
        P.op("pe", fn, [L.cb_r, L.ghi_r, L.glo_r, L.ones1_r], [ps_r])
        P.op("dve", lambda e, c=c, ps=ps: e.tensor_copy(out=L.gam[:, c, :], in_=ps[:, 0:64]), [ps_r], [L.gam_r])
        P.op("dve", lambda e, c=c, ps=ps: e.tensor_tensor(out=L.ekg[:, c, :], in0=ps[:, 64:128], in1=L.gam[:, c, :],
                                                           op=ALU.subtract), [ps_r, L.gam_r], [L.ekg_r])
        _act(P, L.dl[:, c, :], ps[:, 64:128], AF.Exp, [ps_r], [L.dl_r])
    _act(P, L.ekg, L.ekg, AF.Exp, [L.ekg_r], [L.ekg_r])
    _act(P, L.bg, L.gam, AF.Exp, [L.gam_r], [L.bg_r])
    P.op("dve", lambda e: e.tensor_tensor(out=L.bg, in0=L.bg, in1=L.beta, op=ALU.mult), [L.bg_r, L.beta_r], [L.bg_r])
    L.off_end2 = L.off_end


def l1_alloc_job(cx, off):
    P = cx.P
    J = L1()
    o = off

    def mk(name, dt, cols, sz, npar):
        nonlocal o
        aps, regs = [], []
        for p in range(npar):
            a, r = cx.sb(f"{name}{p}", o, cols * sz, dt); o += cols * sz
            if cols == 256:
                a = a.rearrange("p (u n) -> p u n", u=2)
            elif cols == 512:
                a = a.rearrange("p (a u n) -> p a u n", a=2, u=2)
            aps.append(a)
            regs.append(r)
        return aps, regs
    J.E, J.E_r = mk("jE", F32, 256, 4, 3)
    J.T1, J.T1_r = mk("jT1", F32, 256, 4, 3)
    J.Er, J.Er_r = mk("jEr", F32, 256, 4, 3)
    J.KKm, J.KKm_r = mk("jKKm", F32, 128, 4, 3)
    J.NN, J.NN_r = mk("jNN", F32, 512, 4, 3)
    J.Qb, J.Qb_r = mk("jQb", F32, 256, 4, 3)
    J.vb, J.vb_r = mk("jvb", F32, 256, 4, 3)
    J.kb, J.kb_r = mk("jkb", F32, 256, 4, 3)
    J.U, J.U_r = mk("jU", F32, 256, 4, 6)
    J.AT, J.AT_r = mk("jAT", BF16, 256, 2, 6)
    J.qg, J.qg_r = mk("jqg", BF16, 256, 2, 6)
    J.kg, J.kg_r = mk("jkg", BF16, 256, 2, 6)
    J.WT, J.WT_r = mk("jWT", BF16, 256, 2, 6)
    a, J.S_r = cx.sb("jS", o, 1024, F32); o += 1024
    J.S = a.rearrange("p (u n) -> p u n", u=2)
    a, J.Sbf_r = cx.sb("jSbf", o, 512, BF16); o += 512
    J.Sbf = a.rearrange("p (u n) -> p u n", u=2)
    a, J.vn_r = cx.sb("jvn", o, 512, BF16); o += 512
    J.vn = a.rearrange("p (u n) -> p u n", u=2)
    J.off_end = o
    return J


def delta_job(cx, J, d, kh, G, o_sink, extra=None, extra_steps=0):
    P, L = cx.P, cx.L
    order = list(range(NCK)) if d == 0 else list(range(NCK - 1, -1, -1))
    uc = [d * 32 + 2 * kh, d * 32 + 2 * kh + 1]

    def u3(ap):
        return ap.rearrange("p (u n) -> p u n", u=2)

    def pre(c, p, pb, sb):
        cs = slice(c * CH, (c + 1) * CH)
        bkA, bkB = cx.banks[2 * p], cx.banks[2 * p + 1]
        rA, rB = cx.bank_reg[2 * p], cx.bank_reg[2 * p + 1]
        psA, psB = bkA[:, :], bkB[:, :]
        E, T1, Er, KKm = J.E[pb], J.T1[pb], J.Er[pb], J.KKm[pb]
        NN, Qb, vb, kb = J.NN[pb], J.Qb[pb], J.vb[pb], J.kb[pb]
        Nb, NTb = NN[:, 0], NN[:, 1]
        NN_r, Qb_r = J.NN_r[pb], J.Qb_r[pb]

        def s1(e):
            e.matmul(psA[:, 0:128], lhsT=G.kT[:, cs], rhs=G.kT[:, cs], start=True, stop=True)
            ins = e.matmul(psA[:, 128:256], lhsT=G.kT[:, cs], rhs=G.qT[:, cs], start=True, stop=True)
            for u in range(2):
                e.matmul(psA[:, 256 + u * 128:384 + u * 128], lhsT=_bc(L.ghi[:, c, uc[u]:uc[u] + 1], 128),
                         rhs=L.trib[d], start=True, stop=False)
                e.matmul(psA[:, 256 + u * 128:384 + u * 128], lhsT=_bc(L.glo[:, c, uc[u]:uc[u] + 1], 128),
                         rhs=L.trib[d], start=False, stop=True)
                ins = e.matmul(psB[:, u * 128:(u + 1) * 128], lhsT=_bc(L.bb[:, c, uc[u]:uc[u] + 1], 128),
                               rhs=L.identb, start=True, stop=True)
            return ins
        P.op("pe", s1, [G.qk_r, L.ghi_r, L.glo_r, L.bb_r, L.cb_r], [rA, rB])
        yield
        for u in range(2):
            P.op("dve", lambda e, u=u: e.scalar_tensor_tensor(
                out=E[:, u, :], in0=psA[:, 256 + u * 128:384 + u * 128], scalar=L.gam[:, c, uc[u]:uc[u] + 1],
                in1=L.neg[d], op0=ALU.subtract, op1=ALU.add), [rA, L.gam_r, L.cm_r], [J.E_r[pb]])
        P.op("dve", lambda e: e.tensor_tensor(out=KKm, in0=psA[:, 0:128], in1=L.nst[d], op=ALU.mult),
             [rA, L.cm_r], [J.KKm_r[pb]])
        yield
        _act(P, Er, u3(psA[:, 256:512]), AF.Exp, [rA], [J.Er_r[pb]])
        _act(P, E, E, AF.Exp, [J.E_r[pb]], [J.E_r[pb]])
        for u in range(2):
            _act(P, vb[:, u, :], G.v_tm[:, c, u, :], AF.Identity, [G.vtm_r, L.beta_r], [J.vb_r[pb]],
                 scale=L.beta[:, c, uc[u]:uc[u] + 1])
            _act(P, kb[:, u, :], G.k_tm[:, c, :], AF.Identity, [G.ktm_r, L.bg_r], [J.kb_r[pb]],
                 scale=L.bg[:, c, uc[u]:uc[u] + 1])
            _act(P, J.kg[sb][:, u, :], G.k_tm[:, c, :], AF.Identity, [G.ktm_r, L.ekg_r], [J.kg_r[sb]],
                 scale=L.ekg[:, c, uc[u]:uc[u] + 1])
        yield
        for u in range(2):
            P.op("dve", lambda e, u=u: e.tensor_tensor(out=J.AT[sb][:, u, :], in0=psA[:, 128:256], in1=E[:, u, :],
                                                       op=ALU.mult), [rA, J.E_r[pb]], [J.AT_r[sb]])
        P.op("dve", lambda e: e.tensor_tensor(out=T1, in0=u3(psB[:, 0:256]), in1=E, op=ALU.mult),
             [rB, J.E_r[pb]], [J.T1_r[pb]])
        yield
        for u in range(2):
            P.op("pool", lambda e, u=u: e.tensor_tensor(out=NTb[:, u, :], in0=T1[:, u, :], in1=KKm,
                                                        op=ALU.mult), [J.T1_r[pb], J.KKm_r[pb]], [NN_r])
        for u in range(2):
            P.op("pool", lambda e, u=u: e.tensor_tensor(out=Qb[:, u, :], in0=NTb[:, u, :], in1=L.identf,
                                                        op=ALU.add), [NN_r, L.cm_r], [Qb_r])
        for u in range(2):
            P.op("pool", lambda e, u=u: e.tensor_tensor(out=J.qg[sb][:, u, :], in0=Er[:, u, :], in1=G.qT[:, cs],
                                                        op=ALU.mult), [J.Er_r[pb], G.qk_r], [J.qg_r[sb]])
        yield

        def s5(e):
            ins = None
            for u in range(2):
                ins = e.transpose(psB[:, u * 128:(u + 1) * 128], NTb[:, u, :], L.identf)
            return ins
        P.op("pe", s5, [NN_r, L.cm_r], [rB])
        yield
        _act(P, Nb, u3(psB[:, 0:256]), AF.Copy, [rB], [NN_r])
        yield
        for k in range(6):
            last = (k == 5)

            def sq(e):
                ins = None
                for u in range(2):
                    ins = e.matmul(psB[:, u * 128:(u + 1) * 128], lhsT=NTb[:, u, :], rhs=Nb[:, u, :],
                                   start=True, stop=True)
                return ins
            P.op("pe", sq, [NN_r], [rB])
            yield
            _act(P, Nb, u3(psB[:, 0:256]), AF.Copy, [rB], [NN_r])
            yield

            def qq(e, last=last):
                ins = None
                for u in range(2):
                    ins = e.matmul(psA[:, u * 128:(u + 1) * 128], lhsT=Nb[:, u, :], rhs=Qb[:, u, :],
                                   start=True, stop=True)
                if not last:
                    for u in range(2):
                        ins = e.transpose(psB[:, 256 + u * 128:384 + u * 128], Nb[:, u, :], L.identf)
                return ins
            P.op("pe", qq, [NN_r, Qb_r, L.cm_r], [rA] if last else [rA, rB])
            yield
            P.op("dve", lambda e: e.tensor_tensor(out=Qb, in0=u3(psA[:, 0:256]), in1=Qb, op=ALU.add),
                 [rA, Qb_r], [Qb_r])
            if not last:
                _act(P, NTb, u3(psB[:, 256:512]), AF.Copy, [rB], [NN_r])
            yield

        def fin(e):
            ins = None
            for u in range(2):
                e.matmul(psB[:, u * 128:(u + 1) * 128], lhsT=Qb[:, u, :], rhs=vb[:, u, :], start=True, stop=True)
                ins = e.matmul(psB[:, 256 + u * 128:384 + u * 128], lhsT=kb[:, u, :], rhs=Qb[:, u, :],
                               start=True, stop=True)
            return ins
        P.op("pe", fin, [Qb_r, J.vb_r[pb], J.kb_r[pb]], [rB])
        yield
        _act(P, J.U[sb], u3(psB[:, 0:256]), AF.Copy, [rB], [J.U_r[sb]])
        _act(P, J.WT[sb], u3(psB[:, 256:512]), AF.Copy, [rB], [J.WT_r[sb]])
        yield

    pS, rS = cx.banks[6][:, 0:256], cx.bank_reg[6]
    pO, rO = cx.banks[6][:, 256:512], cx.bank_reg[6]
    pD, rD = cx.banks[7][:, 0:256], cx.bank_reg[7]

    def scan(c, sb):
        def ws(e):
            ins = None
            for u in range(2):
                ins = e.matmul(pS[:, u * 128:(u + 1) * 128], lhsT=J.WT[sb][:, u, :], rhs=J.Sbf[:, u, :], start=True, stop=True)
            return ins
        P.op("pe", ws, [J.WT_r[sb], J.Sbf_r], [rS])
        yield
        P.op("dve", lambda e: e.tensor_tensor(out=J.vn, in0=J.U[sb], in1=u3(pS), op=ALU.subtract),
             [J.U_r[sb], rS], [J.vn_r])
        yield

        def oo(e):
            ins = None
            for u in range(2):
                e.matmul(pO[:, u * 128:(u + 1) * 128], lhsT=J.Sbf[:, u, :], rhs=J.qg[sb][:, u, :], start=True, stop=False)
                e.matmul(pO[:, u * 128:(u + 1) * 128], lhsT=J.vn[:, u, :], rhs=J.AT[sb][:, u, :], start=False, stop=True)
            for u in range(2):
                ins = e.matmul(pD[:, u * 128:(u + 1) * 128], lhsT=J.kg[sb][:, u, :], rhs=J.vn[:, u, :], start=True, stop=True)
            return ins
        P.op("pe", oo, [J.Sbf_r, J.qg_r[sb], J.vn_r, J.AT_r[sb], J.kg_r[sb]], [rO, rD])
        yield
        for u in range(2):
            P.op("dve", lambda e, u=u: e.scalar_tensor_tensor(
                out=J.S[:, u, :], in0=J.S[:, u, :], scalar=L.dl[:, c, uc[u]:uc[u] + 1], in1=pD[:, u * 128:(u + 1) * 128],
                op0=ALU.mult, op1=ALU.add), [J.S_r, L.dl_r, rD], [J.S_r])
        _act(P, J.Sbf, J.S, AF.Copy, [J.S_r], [J.Sbf_r])
        o_sink(c, u3(pO), rO)
        yield

    def scans(lst):
        for (c, sb) in lst:
            yield from scan(c, sb)

    def interleave(gens):
        gens = [g for g in gens if g is not None]
        while gens:
            alive = []
            for g in gens:
                try:
                    next(g)
                    alive.append(g)
                except StopIteration:
                    pass
            gens = alive

    prev = []
    n = 0
    grp = 0
    while n < NCK:
        cur = order[n:n + 3]
        gens = []
        lst = []
        for i, c in enumerate(cur):
            sb = (3 * (grp % 2) + i)
            gens.append(pre(c, i, i, sb))
            lst.append((c, sb))
        gens.append(scans(prev) if prev else None)
        if extra is not None:
            def limited(g=extra, k=extra_steps):
                for _ in range(k):
                    try:
                        next(g)
                    except StopIteration:
                        return
                    yield
            gens.append(limited())
        interleave(gens)
        prev = lst
        n += len(cur)
        grp += 1
    interleave([scans(prev)])


NPROJ = 96


def stage_inproj(cx, w_dram, praw, praw_regs):
    P = cx.P
    o = cx.L.off_end
    NW = 3
    wb, wb_r = [], []
    for i in range(NW):
        a, r = cx.sb(f"ip_w{i}", o, KC * 128 * 2, BF16, (KC,)); wb.append(a); wb_r.append(r); o += KC * 128 * 2
    st, st_r = [], []
    for i in range(3):
        a, r = cx.sb(f"ip_st{i}", o, 2048, F32); st.append(a); st_r.append(r); o += 2048

    def wload(j):
        _dma(P, "pool", wb[j % NW], w_dram[j].rearrange("p (k n) -> p k n", k=KC), [], [wb_r[j % NW]], "wld")

    wload(0); wload(1)
    n = 0
    for j in range(NPROJ):
        if j + 2 < NPROJ:
            wload(j + 2)
        w = wb[j % NW]
        tiles = TT_B if j < 64 else TT_M
        for t, (s, l) in enumerate(tiles):
            b = n % 3
            ps, ps_r = cx.bank(n % 3)
            _mm(P, ps[:, :l], [(w[:, kc, :], cx.acta[:, kc, s:s + l]) for kc in range(KC)],
                [wb_r[j % NW], cx.acta_tiles[t]], [ps_r])
            if n % 2 == 0:
                _act(P, st[b][:, :l], ps[:, :l], AF.Copy, [ps_r], [st_r[b]])
            else:
                P.op("dve", lambda e, b=b, l=l, ps=ps: e.tensor_copy(out=st[b][:, :l], in_=ps[:, :l]), [ps_r], [st_r[b]])
            _dma(P, "sp", praw[:, j, s:s + l], st[b][:, :l], [st_r[b]], [praw_regs[j]], "st")
            n += 1


def l1_alloc_group(cx, off, phase):
    P = cx.P
    G = L1()
    o = off

    def mk(name, dt, nbytes, shape=None):
        nonlocal o
        a, r = cx.sb(f"G{phase}_{name}", o, nbytes, dt, shape); o += nbytes
        return a, r
    nset = 2 if phase == "P" else 1
    G.sets = []
    for i in range(nset):
        S_ = L1()
        S_.qk, S_.qk_r = mk(f"qk{i}", BF16, 2 * TOK * 2, (2,))
        S_.qT, S_.kT = S_.qk[:, 0, :], S_.qk[:, 1, :]
        S_.k_tm, S_.ktm_r = mk(f"ktm{i}", BF16, NCK * 128 * 2, (NCK,))
        S_.v_tm, S_.vtm_r = mk(f"vtm{i}", BF16, NCK * 256 * 2, (NCK, 2))
        G.sets.append(S_)
    G.qk, G.qk_r, G.qT, G.kT = G.sets[0].qk, G.sets[0].qk_r, G.sets[0].qT, G.sets[0].kT
    G.k_tm, G.ktm_r, G.v_tm, G.vtm_r = G.sets[0].k_tm, G.sets[0].ktm_r, G.sets[0].v_tm, G.sets[0].vtm_r
    if phase == "P":
        G.rw, G.rw_r = mk("rw", F32, (NTB + 6) * 4)
        G.cv, G.cv_r = mk("cv", F32, TOK * 4)
        G.sqb, G.sqb_r = mk("sqb", BF16, TOK * 2)
        G.rn, G.rn_r = mk("rn", F32, 2048)
        G.vT, G.vT_r = mk("vT", BF16, 2 * TOK * 2, (2,))
        G.ost, G.ost_r = [], []
        for i in range(2):
            a, r = mk(f"ost{i}", F32, 1024, (2,)); G.ost.append(a); G.ost_r.append(r)
    else:
        G.ot, G.ot_r = mk("ot", F32, 2 * TOK * 4, (2,))
        G.z, G.z_r = mk("z", F32, 2 * TOK * 4, (2,))
        G.sq, G.sq_r = mk("sq", BF16, 2 * TOK * 2, (2,))
        G.og, G.og_r = mk("og", BF16, 2 * TOK * 2, (2,))
        G.rn, G.rn_r = mk("rn", F32, 2048)
        G.sl, G.sl_r = mk("sl", F32, 2 * 256 * 4, (2,))
    G.off_end = o
    return G


def group_stage_P(cx, G, kh, praw, praw_regs, scr):
    T = G.sets[kh % 2]
    P, L = cx.P, cx.L
    blocks = [(kh, 0), (16 + kh, 1), (32 + 2 * kh, 2), (32 + 2 * kh + 1, 3)]
    QW = TOK // 4
    for (j, b) in blocks:
        _dma(P, "sp", G.rw[:, 2:2 + NTB], praw[:, j, :], [praw_regs[j]], [G.rw_r], "ld")
        for tap in range(5):
            for q in range(4):
                wcol = cx.pv[:, PV["w_sc"] + j * 5 + tap: PV["w_sc"] + j * 5 + tap + 1]
                osl = slice(q * QW, (q + 1) * QW)
                isl = slice(q * QW + tap, (q + 1) * QW + tap)
                if tap == 0:
                    P.op("dve", lambda e, osl=osl, isl=isl, wcol=wcol: e.tensor_scalar(
                        out=G.cv[:, osl], in0=G.rw[:, isl], scalar1=wcol, scalar2=None, op0=ALU.mult),
                        [G.rw_r, cx.pv_r], [G.cv_r] if q == 0 else [])
                else:
                    P.op("dve", lambda e, osl=osl, isl=isl, wcol=wcol: e.scalar_tensor_tensor(
                        out=G.cv[:, osl], in0=G.rw[:, isl], scalar=wcol, in1=G.cv[:, osl], op0=ALU.mult, op1=ALU.add),
                        [G.rw_r] if (tap == 4 and q == 3) else [], [G.cv_r] if (tap == 4 and q == 3) else [])
            yield
        if b >= 2:
            _act(P, G.vT[:, b - 2, :], G.cv, AF.Silu, [G.cv_r], [G.vT_r])
            yield
        else:
            _act(P, G.cv, G.cv, AF.Silu, [G.cv_r], [G.cv_r])
            _act(P, G.sqb, G.cv, AF.Square, [G.cv_r], [G.sqb_r])
            for t in range(8):
                s, l = t * 256, 256
                ps, ps_r = cx.banks[7][:, 256:512], cx.bank_reg[7]
                _mm(P, ps[:, :l], [(L.ones1, G.sqb[:, s:s + l])], [L.ones1_r, G.sqb_r], [ps_r])
                _act(P, G.rn[:, :l], ps[:, :l], AF.Sqrt, [ps_r, cx.misc_r], [G.rn_r], bias=cx.misc[:, 0:1], scale=1.0)
                P.op("dve", lambda e, l=l: e.reciprocal(out=G.rn[:, :l], in_=G.rn[:, :l]), [G.rn_r], [G.rn_r])
                sc = (128.0 ** -0.5) if b == 0 else 1.0
                P.op("dve", lambda e, s=s, l=l, b=b, sc=sc: e.scalar_tensor_tensor(
                    out=T.qk[:, b, s:s + l], in0=G.cv[:, s:s + l], scalar=sc, in1=G.rn[:, :l], op0=ALU.mult, op1=ALU.mult),
                    [G.cv_r, G.rn_r], [T.qk_r])
                if t % 2 == 1:
                    yield
    for c in range(NCK):
        bank = 7
        psb = cx.banks[bank][:, 256:448].bitcast(BF16)
        ps_r = cx.bank_reg[bank]

        def tr(e, c=c, psb=psb):
            cs = slice(c * CH, (c + 1) * CH)
            e.transpose(psb[:, 0:128], T.kT[:, cs], L.identb)
            e.transpose(psb[:, 128:256], G.vT[:, 0, cs], L.identb)
            return e.transpose(psb[:, 256:384], G.vT[:, 1, cs], L.identb)
        P.op("pe", tr, [T.qk_r, G.vT_r, L.cb_r], [ps_r])
        _act(P, T.k_tm[:, c, :], psb[:, 0:128], AF.Copy, [ps_r], [T.ktm_r])
        P.op("dve", lambda e, c=c, psb=psb: e.tensor_copy(out=T.v_tm[:, c, :, :],
                                                          in_=psb[:, 128:384].rearrange("p (u n) -> p u n", u=2)),
             [ps_r], [T.vtm_r])
        if c % 2 == 1:
            yield
    _dma(P, "sp", scr["qk"][kh], T.qk.rearrange("p a n -> p (a n)"), [T.qk_r], [scr["qk_r"][kh]], "st")
    _dma(P, "sp", scr["ktm"][kh], T.k_tm.rearrange("p a n -> p (a n)"), [T.ktm_r], [scr["ktm_r"][kh]], "st")
    _dma(P, "sp", scr["vtm"][kh], T.v_tm.rearrange("p a b n -> p (a b n)"), [T.vtm_r], [scr["vtm_r"][kh]], "st")


def phase_P(cx, J, G, praw, praw_regs, scr, nkh=NKH):
    P = cx.P
    P.op("pool", lambda e: e.memset(G.rw[:, 0:2], 0.0), [], [G.rw_r])
    for _ in group_stage_P(cx, G, 0, praw, praw_regs, scr):
        pass
    for kh in range(nkh):
        nxt = group_stage_P(cx, G, kh + 1, praw, praw_regs, scr) if kh + 1 < nkh else None
        P.op("pool", lambda e: e.memset(J.S, 0.0), [], [J.S_r])
        P.op("pool", lambda e: e.memset(J.Sbf, 0.0), [], [J.Sbf_r])
        cnt = [0]

        def sink(c, psO, rO, kh=kh, cnt=cnt):
            b = cnt[0] % 2
            cnt[0] += 1
            _act(P, G.ost[b], psO, AF.Copy, [rO], [G.ost_r[b]])
            _dma(P, "sp", scr["o"][kh][:, :, c * CH:(c + 1) * CH], G.ost[b], [G.ost_r[b]], [scr["o_r"][kh]], "st")
        delta_job(cx, J, 0, kh, G.sets[kh % 2], sink, extra=nxt, extra_steps=14)
        if nxt is not None:
            for _ in nxt:
                pass
        _dma(P, "sp", scr["st"][kh], J.S.rearrange("p u n -> p (u n)"), [J.S_r], [scr["st_r"]], "st")


def phase_S(cx, J, G, praw, praw_regs, scr, og_scr, og_regs, nkh=NKH):
    P, L = cx.P, cx.L
    for kh in range(nkh):
        _dma(P, "sp", G.qk.rearrange("p a n -> p (a n)"), scr["qk"][kh], [scr["qk_r"][kh]], [G.qk_r], "ld")
        _dma(P, "sp", G.k_tm.rearrange("p a n -> p (a n)"), scr["ktm"][kh], [scr["ktm_r"][kh]], [G.ktm_r], "ld")
        _dma(P, "sp", G.v_tm.rearrange("p a b n -> p (a b n)"), scr["vtm"][kh], [scr["vtm_r"][kh]], [G.vtm_r], "ld")
        _dma(P, "sp", G.ot, scr["o"][kh], [scr["o_r"][kh]], [G.ot_r], "ld")
        for u in range(2):
            j = 64 + 2 * kh + u
            _dma(P, "sp", G.z[:, u, :], praw[:, j, 0:TOK], [praw_regs[j]], [G.z_r], "ld")
        _dma(P, "sp", G.sl, scr["stall"][:, :, kh, :], [scr["stall_r"]], [G.sl_r], "ld")
        Sf = J.S.rearrange("p u n -> p (u n)")
        P.op("dve", lambda e: e.tensor_scalar(out=Sf, in0=G.sl[:, 0, :], scalar1=pvcol(cx, "msk", 0), scalar2=None,
                                              op0=ALU.mult), [G.sl_r, cx.pv_r], [J.S_r])
        P.op("dve", lambda e: e.scalar_tensor_tensor(out=Sf, in0=G.sl[:, 1, :], scalar=pvcol(cx, "msk", 1), in1=Sf,
                                                     op0=ALU.mult, op1=ALU.add), [G.sl_r, cx.pv_r, J.S_r], [J.S_r])
        _act(P, J.Sbf, J.S, AF.Copy, [J.S_r], [J.Sbf_r])

        def sink(c, psO, rO):
            P.op("dve", lambda e, c=c: e.tensor_tensor(out=G.ot[:, :, c * CH:(c + 1) * CH],
                                                       in0=G.ot[:, :, c * CH:(c + 1) * CH], in1=psO, op=ALU.add),
                 [rO, G.ot_r], [G.ot_r])
        delta_job(cx, J, 1, kh, G.sets[0], sink)
        _act(P, G.sq, G.ot, AF.Square, [G.ot_r], [G.sq_r])
        _act(P, G.z, G.z, AF.Silu, [G.z_r], [G.z_r])
        for u in range(2):
            for t, (s, l) in enumerate(TT_M):
                ps, ps_r = cx.bank(t % 2)
                _mm(P, ps[:, :l], [(L.ones128, G.sq[:, u, s:s + l])], [L.ones128_r, G.sq_r], [ps_r])
                _act(P, G.rn[:, :l], ps[:, :l], AF.Sqrt, [ps_r, cx.misc_r], [G.rn_r], bias=cx.misc[:, 0:1], scale=1.0)
                P.op("dve", lambda e, l=l: e.reciprocal(out=G.rn[:, :l], in_=G.rn[:, :l]), [G.rn_r], [G.rn_r])
                P.op("dve", lambda e, u=u, s=s, l=l: e.scalar_tensor_tensor(
                    out=G.ot[:, u, s:s + l], in0=G.ot[:, u, s:s + l], scalar=pvcol(cx, "dnw"), in1=G.rn[:, :l],
                    op0=ALU.mult, op1=ALU.mult), [G.ot_r, G.rn_r, cx.pv_r], [G.ot_r])
                P.op("dve", lambda e, u=u, s=s, l=l: e.tensor_tensor(out=G.og[:, u, s:s + l], in0=G.ot[:, u, s:s + l],
                                                                     in1=G.z[:, u, s:s + l], op=ALU.mult),
                     [G.ot_r, G.z_r], [G.og_r])
            _dma(P, "sp", og_scr[:, 2 * kh + u, :], G.og[:, u, :], [G.og_r], [og_regs[2 * kh + u]], "st")
```

```python
import numpy as np
import concourse.bass as bass
import concourse.mybir as mybir
from concourse.bass_utils import run_bass_kernel_spmd

F32 = mybir.dt.float32
BF16 = mybir.dt.bfloat16
AF = mybir.ActivationFunctionType
ALU = mybir.AluOpType
AX = mybir.AxisListType

ENGS = ("pe", "act", "dve", "pool", "sp")
EPOCH = 24000


class Region:
    __slots__ = ("name", "arena", "lo", "hi", "last_w", "readers", "pending", "parent", "excl")

    def __init__(self, name, arena=None, lo=0, hi=0):
        self.parent = None
        self.excl = False
        self.name = name
        self.arena = arena
        self.lo = lo
        self.hi = hi
        self.last_w = None
        self.readers = []
        self.pending = []


class Op:
    __slots__ = ("eng", "fn", "pos", "gidx", "deps", "is_dma", "dma_sem", "dma_val",
                 "waits", "signal", "signo", "clock", "dclock", "inc")

    def __init__(self, eng, fn, is_dma):
        self.eng = eng
        self.fn = fn
        self.is_dma = is_dma
        self.deps = []
        self.waits = []
        self.signal = False
        self.signo = 0
        self.dma_sem = None
        self.dma_val = 0
        self.inc = 16


class Prog:
    def __init__(self):
        self.ops = {e: [] for e in ENGS}
        self.all_ops = []
        self.arena_regions = {}
        self.dma_groups = {}

    def region(self, name, arena=None, lo=0, hi=0, parent=None):
        r = Region(name, arena, lo, hi)
        if parent is not None:
            r.parent = parent
            r.pending = list(parent.pending) + ([parent.last_w] if parent.last_w is not None else []) \
                + list(parent.readers)
        if arena is not None:
            lst = self.arena_regions.setdefault(arena, [])
            keep = []
            for o in lst:
                if o.lo < hi and lo < o.hi:
                    if o.last_w is not None:
                        r.pending.append(o.last_w)
                    r.pending.extend(o.readers)
                    r.pending.extend(o.pending)
                    if not (lo <= o.lo and o.hi <= hi):
                        keep.append(o)
                else:
                    keep.append(o)
            keep.append(r)
            self.arena_regions[arena] = keep
        return r

    def _add(self, eng, fn, reads, writes, is_dma, sem_group=None):
        op = Op(eng, fn, is_dma)
        deps = {}
        for r in reads:
            if r.last_w is not None:
                deps[id(r.last_w)] = r.last_w
            if r.excl:
                for rd in r.readers:
                    if rd.eng != eng:
                        deps[id(rd)] = rd
            for p in r.pending:
                deps[id(p)] = p
        for w in writes:
            if w.last_w is not None:
                deps[id(w.last_w)] = w.last_w
            for rd in w.readers:
                deps[id(rd)] = rd
            for p in w.pending:
                deps[id(p)] = p
        if sem_group is not None:
            g = self.dma_groups[sem_group]
            slot = g["n"] % len(g["last"])
            prev = g["last"][slot]
            if prev is not None:
                deps[id(prev)] = prev
            g["last"][slot] = op
            g["cnt"][slot] += 1
            op.dma_sem = (sem_group, slot)
            op.dma_val = 16 * g["cnt"][slot]
            g["n"] += 1
        deps.pop(id(op), None)
        op.deps = list(deps.values())
        for r in reads:
            if not is_dma:
                r.readers = [o for o in r.readers if o.is_dma or o.eng != eng]
            r.readers.append(op)
        for w in writes:
            w.last_w = op
            w.readers = []
            w.pending = []
        for x in list(reads) + list(writes):
            pr = x.parent
            if pr is not None:
                if not is_dma:
                    pr.readers = [o for o in pr.readers if o.is_dma or o.eng != eng]
                pr.readers.append(op)
        op.pos = len(self.ops[eng])
        op.gidx = len(self.all_ops)
        self.ops[eng].append(op)
        self.all_ops.append(op)
        return op

    def op(self, eng, fn, reads=(), writes=()):
        return self._add(eng, fn, reads, writes, False)

    def wait_ops(self, eng, deps):
        op = Op(eng, lambda e: None, False)
        op.deps = list(deps)
        op.pos = len(self.ops[eng])
        op.gidx = len(self.all_ops)
        self.ops[eng].append(op)
        self.all_ops.append(op)
        return op

    def dma_group(self, name, nslots):
        self.dma_groups[name] = {"n": 0, "last": [None] * nslots, "cnt": [0] * nslots}

    def dma(self, eng, fn, reads=(), writes=(), group=None, inc=16):
        assert group in self.dma_groups, group
        op = self._add(eng, fn, reads, writes, True, group)
        if inc != 16:
            op.dma_val = (op.dma_val // 16) * inc
        op.inc = inc
        return op

    def plan(self):
        known = {e: {f: -1 for f in ENGS} for e in ENGS}
        dknown = {e: {} for e in ENGS}
        for op in self.all_ops:
            E = op.eng
            kn, dk = known[E], dknown[E]
            for d in sorted(op.deps, key=lambda o: -o.gidx):
                if d.is_dma:
                    if dk.get(d.dma_sem, 0) >= d.dma_val:
                        continue
                    op.waits.append(d)
                else:
                    if d.eng == E and E == "pe":
                        continue
                    if kn[d.eng] >= d.pos:
                        continue
                    op.waits.append(d)
                    d.signal = True
                for f in ENGS:
                    if d.clock[f] > kn[f]:
                        kn[f] = d.clock[f]
                for s, v in d.dclock.items():
                    if dk.get(s, 0) < v:
                        dk[s] = v
            clock = dict(kn)
            dclock = dict(dk)
            if op.is_dma:
                dclock[op.dma_sem] = max(dclock.get(op.dma_sem, 0), op.dma_val)
            else:
                clock[E] = max(clock[E], op.pos)
            op.clock = clock
            op.dclock = dclock
        for e in ENGS:
            n = 0
            for op in self.ops[e]:
                if op.signal and not op.is_dma:
                    n += 1
                    op.signo = n

    def finish(self):
        lasts = [self.ops[e][-1] for e in ENGS if e != "sp" and self.ops[e]]
        self.wait_ops("sp", lasts)

    def emit(self, nc, stack):
        self.finish()
        self.plan()
        nsig = {e: sum(1 for o in self.ops[e] if o.signal and not o.is_dma) for e in ENGS}
        esems = {}
        for e in ENGS:
            k = max(1, (nsig[e] + EPOCH - 1) // EPOCH)
            esems[e] = [stack.enter_context(nc.semaphore(f"q_{e}_{i}")) for i in range(k)]
        dsems = {}
        for gname, g in self.dma_groups.items():
            for slot in range(len(g["last"])):
                dsems[(gname, slot)] = stack.enter_context(nc.semaphore(f"d_{gname}_{slot}"))
        block = stack.enter_context(nc.Block())

        def run(e, eng):
            for op in self.ops[e]:
                for d in op.waits:
                    if d.is_dma:
                        eng.wait_ge(dsems[d.dma_sem], d.dma_val)
                    else:
                        ep = (d.signo - 1) // EPOCH
                        eng.wait_ge(esems[d.eng][ep], d.signo - ep * EPOCH)
                ins = op.fn(eng)
                if ins is None:
                    assert not op.signal and not op.is_dma
                    continue
                if op.is_dma:
                    ins.then_inc(dsems[op.dma_sem], op.inc)
                elif op.signal:
                    ep = (op.signo - 1) // EPOCH
                    ins.then_inc(esems[e][ep], 1)

        @block.tensor
        def _(eng):
            run("pe", eng)

        @block.scalar
        def _(eng):
            run("act", eng)

        @block.vector
        def _(eng):
            run("dve", eng)

        @block.gpsimd
        def _(eng):
            run("pool", eng)

        @block.sync
        def _(eng):
            run("sp", eng)


D = 2048
KC = 16
SEQ = 4096
BATCH = 4
TOK = 2048
EXT = 2
CPAD = 15
NTA = TOK + EXT + CPAD
NTB = TOK + EXT
FFH = 5632
FC = 44
RMS_EPS = 1e-6
LN_EPS = 1e-5

TT_A = [(0, 512), (512, 512), (1024, 512), (1536, 512), (2048, NTA - 2048)]
TT_B = [(0, 512), (512, 512), (1024, 512), (1536, 512), (2048, EXT)]
TT_M = [(0, 512), (512, 512), (1024, 512), (1536, 512)]

PV = {}
_c = 0
for _n, _w in [("mixn0", 16), ("ffnn0", 16), ("mixn1", 16), ("ffnn1", 16), ("finn", 16),
               ("b_pw1", 32), ("b_dw", 16), ("ln_g", 16), ("ln_b", 16), ("b_pw2", 16),
               ("w_dw", 16 * 31), ("w_sc", 64 * 5), ("dnw", 1), ("alog", 1), ("dtb", 1),
               ("msk", 2)]:
    PV[_n] = _c
    _c += _w
NPV = _c


class Ctx:
    def __init__(self, nc, P, stack):
        self.nc, self.P, self.stack = nc, P, stack
        self.ARENA_F = 50176
        self.arena = stack.enter_context(nc.sbuf_tensor("arena", [128, self.ARENA_F], F32))
        self.banks = [stack.enter_context(nc.psum_tensor(f"bank{i}", [128, 512], F32)) for i in range(8)]
        self.bank_reg = [P.region(f"bank{i}") for i in range(8)]
        for r in self.bank_reg:
            r.excl = True

    def sb(self, name, off, nbytes, dtype, shape=None):
        assert off % 4 == 0 and nbytes % 4 == 0, (off, nbytes)
        assert off + nbytes <= self.ARENA_F * 4, (name, off, nbytes)
        ap = self.arena[:, off // 4:(off + nbytes) // 4]
        if dtype != F32:
            ap = ap.bitcast(dtype)
        if shape is not None:
            if len(shape) == 1:
                ap = ap.rearrange("p (a n) -> p a n", a=shape[0])
            elif len(shape) == 2:
                ap = ap.rearrange("p (a b n) -> p a b n", a=shape[0], b=shape[1])
        reg = self.P.region(name, "arena", off, off + nbytes)
        return ap, reg

    def bank(self, i):
        return self.banks[i][:, :], self.bank_reg[i]


def _dma(P, eng, out, in_, reads, writes, group):
    return P.dma(eng, lambda e, o=out, i=in_: e.dma_start(out=o, in_=i), reads, writes, group)


def _mm(P, ps, pairs, reads, writes):
    def fn(e, ps=ps, pairs=pairs):
        n = len(pairs)
        ins = None
        for i, (l, r) in enumerate(pairs):
            ins = e.matmul(ps, lhsT=l, rhs=r, start=(i == 0), stop=(i == n - 1))
        return ins
    return P.op("pe", fn, reads, writes)


def _act(P, out, in_, func, reads, writes, bias=None, scale=None):
    kw = {}
    if bias is not None:
        kw["bias"] = bias
    if scale is not None:
        kw["scale"] = scale
    return P.op("act", lambda e, o=out, i=in_, f=func, kw=kw: e.activation(out=o, in_=i, func=f, **kw), reads, writes)


OFF_CONST = 0
SZ_PV = NPV * 4
OFF_ONES = ((SZ_PV + 63) // 64) * 64
OFF_MISC = OFF_ONES + 256
OFF_ACTA = 8192
ACT_COLS = 2080
SZ_ACTA = KC * ACT_COLS * 2
OFF_WORK = OFF_ACTA + SZ_ACTA


def setup_consts(cx, pv_dram):
    P = cx.P
    cx.pv, cx.pv_r = cx.sb("pv", OFF_CONST, SZ_PV, F32)
    cx.ones, cx.ones_r = cx.sb("ones", OFF_ONES, 256, BF16)
    cx.misc, cx.misc_r = cx.sb("misc", OFF_MISC, 64, F32)
    _dma(P, "sp", cx.pv, pv_dram, [], [cx.pv_r], "ld")
    P.op("pool", lambda e: e.memset(cx.ones, 1.0 / D), [], [cx.ones_r])
    P.op("pool", lambda e: e.memset(cx.misc[:, 0:1], RMS_EPS), [], [cx.misc_r])
    P.op("pool", lambda e: e.memset(cx.misc[:, 1:2], LN_EPS), [cx.misc_r], [cx.misc_r])
    P.op("pool", lambda e: e.memset(cx.misc[:, 2:3], 1.0), [cx.misc_r], [cx.misc_r])
    cx.acta, cx.acta_r = cx.sb("acta", OFF_ACTA, SZ_ACTA, BF16, (KC,))


def pvcol(cx, name, i=0, n=1):
    c = PV[name] + i
    return cx.pv[:, c:c + n]


def stage_rmsnorm(cx, src, tiles, gname, tag, src_regs=lambda t: [], out_dst=None, out_regs=None,
                  work_off=None, nbuf=2):
    P = cx.P
    o = OFF_WORK if work_off is None else work_off
    xt, xt_r = [], []
    for b in range(nbuf):
        a, r = cx.sb(f"{tag}_xt{b}", o, KC * 512 * 4, F32, (KC,))
        xt.append(a); xt_r.append(r); o += KC * 512 * 4
    sq, sq_r = cx.sb(f"{tag}_sq", o, KC * 512 * 2, BF16, (KC,)); o += KC * 512 * 2
    rt, rt_r = cx.sb(f"{tag}_rt", o, 512 * 4, F32); o += 512 * 4
    if out_dst is not None:
        yt, yt_r = cx.sb(f"{tag}_yt", OFF_ACTA, KC * 512 * 4, F32, (KC,))
    ps, ps_r = cx.bank(7)
    if out_dst is None:
        cx.acta, cx.acta_r = cx.sb(f"{tag}_acta", OFF_ACTA, SZ_ACTA, BF16, (KC,))
    acta_t = [P.region(f"{tag}_acta{t}", parent=cx.acta_r) for t in range(len(tiles))]
    cx.acta_tiles = acta_t
    loads = {}

    def load(t):
        s, l = tiles[t]
        loads[t] = _dma(P, "sp", xt[t % nbuf][:, :, :l], src[:, :, s:s + l], src_regs(t), [xt_r[t % nbuf]], "ld")

    if nbuf > 1:
        load(0)
    for t, (s, l) in enumerate(tiles):
        b = t % nbuf
        if nbuf == 1:
            load(t)
        elif t + 1 < len(tiles):
            load(t + 1)
        _act(P, sq[:, :, :l], xt[b][:, :, :l], AF.Square, [xt_r[b]], [sq_r])
        _mm(P, ps[:, :l], [(cx.ones, sq[:, kc, :l]) for kc in range(KC)], [cx.ones_r, sq_r], [ps_r])
        _act(P, rt[:, :l], ps[:, :l], AF.Sqrt, [ps_r, cx.misc_r], [rt_r], bias=cx.misc[:, 0:1], scale=1.0)
        P.op("dve", lambda e, l=l: e.reciprocal(out=rt[:, :l], in_=rt[:, :l]), [rt_r], [rt_r])
        if out_dst is not None:
            for kc in range(KC):
                P.op("dve", lambda e, kc=kc, b=b, l=l: e.scalar_tensor_tensor(
                    out=yt[:, kc, :l], in0=xt[b][:, kc, :l], scalar=pvcol(cx, gname, kc),
                    in1=rt[:, :l], op0=ALU.mult, op1=ALU.mult),
                    [xt_r[b], rt_r, cx.pv_r], [yt_r])
            _dma(P, "sp", out_dst[:, :, s:s + l], yt[:, :, :l], [yt_r], [out_regs[t]], "st")
            continue
        for kc in range(KC):
            P.op("dve", lambda e, kc=kc, b=b, s=s, l=l: e.scalar_tensor_tensor(
                out=cx.acta[:, kc, s:s + l], in0=xt[b][:, kc, :l], scalar=pvcol(cx, gname, kc),
                in1=rt[:, :l], op0=ALU.mult, op1=ALU.mult),
                [xt_r[b], rt_r, cx.pv_r], [acta_t[t]])


def stage_pw1(cx, w_dram, u_scr, u_regs):
    P = cx.P
    o = OFF_WORK
    NW = 3
    wb, wb_r = [], []
    for i in range(NW):
        a, r = cx.sb(f"pw1_w{i}", o, KC * 256 * 2, BF16, (KC,)); wb.append(a); wb_r.append(r); o += KC * 256 * 2
    sg, sg_r, ut, ut_r = [], [], [], []
    for i in range(2):
        a, r = cx.sb(f"pw1_sg{i}", o, 2048, F32); sg.append(a); sg_r.append(r); o += 2048
        a, r = cx.sb(f"pw1_ut{i}", o, 2048, F32); ut.append(a); ut_r.append(r); o += 2048
    zt, zt_r = cx.sb("pw1_z", o, KC * 16 * 4, F32, (KC,)); o += KC * 16 * 4
    P.op("pool", lambda e: e.memset(zt, 0.0), [], [zt_r])
    _dma(P, "sp", u_scr[:, :, 0:CPAD], zt[:, :, 0:CPAD], [zt_r], [u_regs["pad"]], "st")
    tiles = TT_A
    n = 0
    wl = {}

    def wload(j):
        wl[j] = _dma(P, "pool", wb[j % NW], w_dram[j].rearrange("p (k n) -> p k n", k=KC), [], [wb_r[j % NW]], "wld")

    wload(0); wload(1)
    for j in range(KC):
        if j + 2 < KC:
            wload(j + 2)
        w = wb[j % NW]
        for t, (s, l) in enumerate(tiles):
            pa, pa_r = cx.bank((n % 2) * 2)
            pg, pg_r = cx.bank((n % 2) * 2 + 1)
            b = n % 2
            rd = [wb_r[j % NW], cx.acta_tiles[t]]
            _mm(P, pa[:, :l], [(w[:, kc, 0:128], cx.acta[:, kc, s:s + l]) for kc in range(KC)], rd, [pa_r])
            _mm(P, pg[:, :l], [(w[:, kc, 128:256], cx.acta[:, kc, s:s + l]) for kc in range(KC)], rd, [pg_r])
            _act(P, sg[b][:, :l], pg[:, :l], AF.Sigmoid, [pg_r, cx.pv_r], [sg_r[b]],
                 bias=pvcol(cx, "b_pw1", 16 + j), scale=1.0)
            P.op("dve", lambda e, b=b, l=l, pa=pa, j=j: e.scalar_tensor_tensor(
                out=ut[b][:, :l], in0=pa[:, :l], scalar=pvcol(cx, "b_pw1", j), in1=sg[b][:, :l],
                op0=ALU.add, op1=ALU.mult), [pa_r, sg_r[b], cx.pv_r], [ut_r[b]])
            _dma(P, "sp", u_scr[:, j, CPAD + s:CPAD + s + l], ut[b][:, :l], [ut_r[b]], [u_regs[(j, t)]], "st")
            n += 1


def stage_conv_ln(cx, u_scr, u_regs):
    P = cx.P
    TW = 256
    tiles = [(s, TW) for s in range(0, TOK - TW, TW)] + [(TOK - TW, TW + EXT)]
    LMAX = TW + EXT
    UW = LMAX + 2 * CPAD
    o = OFF_WORK
    uh, uh_r = [], []
    for i in range(2):
        a, r = cx.sb(f"cv_uh{i}", o, KC * UW * 4, F32, (KC,)); uh.append(a); uh_r.append(r); o += KC * UW * 4
    co, co_r = cx.sb("cv_co", o, KC * LMAX * 4, F32, (KC,)); o += KC * LMAX * 4
    cb, cb_r = cx.sb("cv_cb", o, KC * LMAX * 2, BF16, (KC,)); o += KC * LMAX * 2
    sqb, sqb_r = cx.sb("cv_sqb", o, KC * LMAX * 2, BF16, (KC,)); o += KC * LMAX * 2
    mean, mean_r = cx.sb("cv_mean", o, LMAX * 4, F32); o += LMAX * 4
    var, var_r = cx.sb("cv_var", o, LMAX * 4, F32); o += LMAX * 4
    pm, pm_r = cx.bank(4)
    pq, pq_r = cx.bank(5)
    cx.acta, cx.acta_r = cx.sb("cv_actar", OFF_ACTA, SZ_ACTA, BF16, (KC,))
    acta_t = [P.region(f"cv_acta{t}", parent=cx.acta_r) for t in range(len(tiles))]
    co_k = [P.region(f"cv_co{k}", parent=co_r) for k in range(KC)]
    all_u = list(u_regs.values())

    def load(t):
        s, l = tiles[t]
        _dma(P, "sp", uh[t % 2][:, :, :l + 2 * CPAD], u_scr[:, :, s:s + l + 2 * CPAD], all_u, [uh_r[t % 2]], "ld")

    load(0)
    NCH = 4
    for t, (s, l) in enumerate(tiles):
        b = t % 2
        if t + 1 < len(tiles):
            load(t + 1)
        for k0 in range(0, KC, NCH):
            for j in range(31):
                for kc in range(k0, k0 + NCH):
                    wcol = cx.pv[:, PV["w_dw"] + kc * 31 + j: PV["w_dw"] + kc * 31 + j + 1]
                    if j == 0:
                        P.op("dve", lambda e, kc=kc, b=b, l=l, wcol=wcol: e.tensor_scalar(
                            out=co[:, kc, :l], in0=uh[b][:, kc, 0:l], scalar1=wcol, scalar2=pvcol(cx, "b_dw", kc),
                            op0=ALU.mult, op1=ALU.add), [uh_r[b], cx.pv_r], [co_k[kc]])
                    else:
                        P.op("dve", lambda e, kc=kc, b=b, l=l, j=j, wcol=wcol: e.scalar_tensor_tensor(
                            out=co[:, kc, :l], in0=uh[b][:, kc, j:j + l], scalar=wcol, in1=co[:, kc, :l],
                            op0=ALU.mult, op1=ALU.add), [uh_r[b], cx.pv_r] if j == 30 else [],
                            [co_k[kc]] if j == 30 else [])
        _act(P, cb[:, :, :l], co[:, :, :l], AF.Copy, co_k, [cb_r])
        _act(P, sqb[:, :, :l], co[:, :, :l], AF.Square, co_k, [sqb_r])
        _mm(P, pm[:, :l], [(cx.ones, cb[:, kc, :l]) for kc in range(KC)], [cx.ones_r, cb_r], [pm_r])
        _mm(P, pq[:, :l], [(cx.ones, sqb[:, kc, :l]) for kc in range(KC)], [cx.ones_r, sqb_r], [pq_r])
        _act(P, mean[:, :l], pm[:, :l], AF.Copy, [pm_r], [mean_r])
        P.op("dve", lambda e, l=l: e.tensor_tensor(out=var[:, :l], in0=mean[:, :l], in1=mean[:, :l], op=ALU.mult),
             [mean_r], [var_r])
        P.op("dve", lambda e, l=l: e.tensor_tensor(out=var[:, :l], in0=pq[:, :l], in1=var[:, :l], op=ALU.subtract),
             [pq_r, var_r], [var_r])
        _act(P, var[:, :l], var[:, :l], AF.Sqrt, [var_r, cx.misc_r], [var_r], bias=cx.misc[:, 1:2], scale=1.0)
        P.op("dve", lambda e, l=l: e.reciprocal(out=var[:, :l], in_=var[:, :l]), [var_r], [var_r])
        for kc in range(KC):
            P.op("pool", lambda e, kc=kc, l=l: e.tensor_tensor(out=co[:, kc, :l], in0=co[:, kc, :l], in1=mean[:, :l],
                                                               op=ALU.subtract), [co_k[kc], mean_r], [co_k[kc]])
            P.op("dve", lambda e, kc=kc, l=l: e.tensor_tensor(out=co[:, kc, :l], in0=co[:, kc, :l], in1=var[:, :l],
                                                              op=ALU.mult), [co_k[kc], var_r], [co_k[kc]])
            _act(P, cx.acta[:, kc, s:s + l], co[:, kc, :l], AF.Silu, [co_k[kc], cx.pv_r], [acta_t[t]],
                 bias=pvcol(cx, "ln_b", kc), scale=pvcol(cx, "ln_g", kc))
    cx.acta_tiles_fine = (tiles, acta_t)


def stage_proj_res(cx, w_dram, nblk, bw, kchunks, tiles, act_reads, bias_name, res_src, res_regs, dst, dst_regs,
                   tag, act_ap=None):
    P = cx.P
    o = cx.work_off
    NW = 3
    wb, wb_r = [], []
    for i in range(NW):
        a, r = cx.sb(f"{tag}_w{i}", o, kchunks * bw * 2, BF16, (kchunks,)); wb.append(a); wb_r.append(r)
        o += kchunks * bw * 2
    rs, rs_r, ot, ot_r = [], [], [], []
    for i in range(3):
        a, r = cx.sb(f"{tag}_rs{i}", o, 2048, F32); rs.append(a); rs_r.append(r); o += 2048
        a, r = cx.sb(f"{tag}_ot{i}", o, 2048, F32); ot.append(a); ot_r.append(r); o += 2048
    act = cx.acta if act_ap is None else act_ap

    def wload(j):
        _dma(P, "pool", wb[j % NW], w_dram[j].rearrange("p (k n) -> p k n", k=kchunks), [], [wb_r[j % NW]], "wld")

    wload(0)
    if nblk > 1:
        wload(1)
    n = 0
    for j in range(nblk):
        if j + 2 < nblk:
            wload(j + 2)
        w = wb[j % NW]
        for t, (s, l, a0) in enumerate(tiles):
            b = n % 3
            ps, ps_r = cx.bank(n % 3)
            _dma(P, "sp", rs[b][:, :l], res_src[:, j, s:s + l], [res_regs[(j, t)]], [rs_r[b]], "ld")
            _mm(P, ps[:, :l], [(w[:, kc, :], act[:, kc, a0:a0 + l]) for kc in range(kchunks)],
                [wb_r[j % NW]] + act_reads(t), [ps_r])
            if bias_name is not None:
                P.op("dve", lambda e, b=b, l=l, ps=ps, j=j: e.scalar_tensor_tensor(
                    out=ot[b][:, :l], in0=ps[:, :l], scalar=pvcol(cx, bias_name, j), in1=rs[b][:, :l],
                    op0=ALU.add, op1=ALU.add), [ps_r, rs_r[b], cx.pv_r], [ot_r[b]])
            else:
                P.op("dve", lambda e, b=b, l=l, ps=ps: e.tensor_tensor(
                    out=ot[b][:, :l], in0=ps[:, :l], in1=rs[b][:, :l], op=ALU.add), [ps_r, rs_r[b]], [ot_r[b]])
            _dma(P, "sp", dst[:, j, s:s + l], ot[b][:, :l], [ot_r[b]], [dst_regs[(j, t)]], "st")
            n += 1


def stage_gate_up(cx, w_dram, hid_scr, hid_regs, tiles):
    P = cx.P
    o = OFF_WORK
    NW = 3
    wb, wb_r = [], []
    for i in range(NW):
        a, r = cx.sb(f"gu_w{i}", o, KC * 256 * 2, BF16, (KC,)); wb.append(a); wb_r.append(r); o += KC * 256 * 2
    sg, sg_r, ht, ht_r = [], [], [], []
    for i in range(2):
        a, r = cx.sb(f"gu_sg{i}", o, 2048, F32); sg.append(a); sg_r.append(r); o += 2048
        a, r = cx.sb(f"gu_ht{i}", o, 1024, BF16); ht.append(a); ht_r.append(r); o += 1024

    def wload(j):
        _dma(P, "pool", wb[j % NW], w_dram[j].rearrange("p (k n) -> p k n", k=KC), [], [wb_r[j % NW]], "wld")

    wload(0); wload(1)
    n = 0
    for j in range(FC):
        if j + 2 < FC:
            wload(j + 2)
        w = wb[j % NW]
        for t, (s, l) in enumerate(tiles):
            b = n % 2
            pg, pg_r = cx.bank((n % 2) * 2)
            pu, pu_r = cx.bank((n % 2) * 2 + 1)
            rd = [wb_r[j % NW], cx.acta_tiles[t]]
            _mm(P, pg[:, :l], [(w[:, kc, 0:128], cx.acta[:, kc, s:s + l]) for kc in range(KC)], rd, [pg_r])
            _mm(P, pu[:, :l], [(w[:, kc, 128:256], cx.acta[:, kc, s:s + l]) for kc in range(KC)], rd, [pu_r])
            _act(P, sg[b][:, :l], pg[:, :l], AF.Silu, [pg_r], [sg_r[b]])
            P.op("dve", lambda e, b=b, l=l, pu=pu: e.tensor_tensor(out=ht[b][:, :l], in0=pu[:, :l], in1=sg[b][:, :l],
                                                                   op=ALU.mult), [pu_r, sg_r[b]], [ht_r[b]])
            _dma(P, "sp", hid_scr[:, j, s:s + l], ht[b][:, :l], [ht_r[b]], [hid_regs[(j, t)]], "st")
            n += 1


def stage_down(cx, w_dram, hid_scr, hid_regs, h_scr, h_regs, tiles, FC=FC, tag="dn"):
    P = cx.P
    passes = [[0, 1], [2, 3] + ([4] if len(tiles) > 4 else [])]
    for pi, tl in enumerate(passes):
        s0 = tiles[tl[0]][0]
        ncol = sum(tiles[t][1] for t in tl)
        hid, hid_r = cx.sb(f"{tag}_hid{pi}", OFF_ACTA, FC * 1028 * 2, BF16, (FC,))
        hr = [P.region(f"{tag}_hid{pi}_{t}", parent=hid_r) for t in tl]
        GS = 11 if FC % 11 == 0 else 8
        for ti, t in enumerate(tl):
            s, l = tiles[t]
            for g in range(0, FC, GS):
                _dma(P, "sp", hid[:, g:g + GS, s - s0:s - s0 + l], hid_scr[:, g:g + GS, s:s + l],
                     [hid_regs[(j, t)] for j in range(g, g + GS)], [hr[ti]], "ld")
        cx.work_off = OFF_ACTA + FC * 1028 * 2
        stage_proj_res(cx, w_dram, KC, 128, FC, [(tiles[t][0], tiles[t][1], tiles[t][0] - s0) for t in tl],
                       lambda t, hr=hr: [hr[t]], None, h_scr,
                       {(j, ti): h_regs[(j, t)] for j in range(KC) for ti, t in enumerate(tl)}, h_scr,
                       {(j, ti): h_regs[(j, t)] for j in range(KC) for ti, t in enumerate(tl)}, f"{tag}{pi}", act_ap=hid)


def dview(t, k):
    return t.ap().rearrange("(k p) n -> p k n", p=128)


def build_program(mode="full", ncores=8):
    from contextlib import ExitStack
    nc = bass.Bass("TRN2", target_bir_lowering=False)
    P = Prog()
    P.dma_group("ld", 6)
    P.dma_group("st", 6)
    P.dma_group("wld", 3)
    P.dma_group("cc", 1)
    xT = nc.dram_tensor("xT", [D, NTA], F32, kind="ExternalInput")
    pv = nc.dram_tensor("pv", [128, NPV], F32, kind="ExternalInput")
    w_pw1 = nc.dram_tensor("w_pw1", [KC, 128, KC * 256], F32, kind="ExternalInput")
    w_pw2 = nc.dram_tensor("w_pw2", [KC, 128, KC * 128], F32, kind="ExternalInput")
    w_gu0 = nc.dram_tensor("w_gu0", [FC, 128, KC * 256], F32, kind="ExternalInput")
    w_dn0 = nc.dram_tensor("w_dn0", [KC, 128, FC * 128], F32, kind="ExternalInput")
    u_scr = nc.dram_tensor("u_scr", [D, CPAD + NTA], F32, kind="ExternalOutput" if mode == "l0a" else "Internal")
    hid_scr = nc.dram_tensor("hid_scr", [FFH, NTB], BF16, kind="Internal")
    if mode in ("l0", "l0a"):
        h_scr = nc.dram_tensor("hout", [D, NTB], F32, kind="ExternalOutput")
    else:
        h_scr = nc.dram_tensor("h_scr", [D, NTB], F32, kind="Internal")
    xv, uv, hv, hidv = dview(xT, KC), dview(u_scr, KC), dview(h_scr, KC), dview(hid_scr, FC)
    full = mode not in ("l0", "l0a", "l0b")
    if full:
        cmat = nc.dram_tensor("cmat", [128, NCM], F32, kind="ExternalInput")
        lvl = {"dbg_l0i": -1, "dbg_l1s": -1, "dbg_n2": 0, "dbg_g": 1, "dbg_ip": 2, "dbg_P": 3, "dbg_X": 3, "dbg_S": 3}.get(mode, 9)
        w_gate = nc.dram_tensor("w_gate", [128, KC * 128], F32, kind="ExternalInput") if lvl >= 1 else None
        w_in = nc.dram_tensor("w_in", [NPROJ, 128, KC * 128], F32, kind="ExternalInput") if lvl >= 2 else None
        if lvl >= 9:
            w_out = nc.dram_tensor("w_out", [KC, 128, 32 * 128], F32, kind="ExternalInput")
            w_gu1 = nc.dram_tensor("w_gu1", [FC, 128, KC * 256], F32, kind="ExternalInput")
            w_dn1 = nc.dram_tensor("w_dn1", [KC, 128, FC * 128], F32, kind="ExternalInput")
        dbg = mode.startswith("dbg")
        DK = "ExternalOutput" if (dbg and lvl >= 3) else "Internal"
        praw_t = nc.dram_tensor("praw", [NPROJ * 128, NTB], F32, kind="ExternalOutput" if (dbg and lvl >= 2) else "Internal")
        qk_t = nc.dram_tensor("qk_scr", [NKH, 128, 2 * TOK], BF16, kind=DK)
        ktm_t = nc.dram_tensor("ktm_scr", [NKH, 128, NCK * 128], BF16, kind=DK)
        vtm_t = nc.dram_tensor("vtm_scr", [NKH, 128, NCK * 256], BF16, kind=DK)
        o_t = nc.dram_tensor("o_scr", [NKH, 128, 2 * TOK], F32, kind=DK)
        st_t = nc.dram_tensor("st_scr", [NKH * 128, 256], F32, kind="Internal")
        stall_t = nc.dram_tensor("st_all", [2 * NKH * 128, 256], F32, kind="Internal")
        og_t = nc.dram_tensor("og_scr", [VD, TOK], BF16, kind=DK)
        if not dbg:
            out_t = nc.dram_tensor("out", [D, TOK], F32, kind="ExternalOutput")

    with ExitStack() as stack:
        cx = Ctx(nc, P, stack)
        setup_consts(cx, pv.ap())
        u_regs = {(j, t): P.region(f"u{j}_{t}") for j in range(KC) for t in range(len(TT_A))}
        u_regs["pad"] = P.region("upad")
        h_regs = {(j, t): P.region(f"h{j}_{t}") for j in range(KC) for t in range(len(TT_B))}
        hid_regs = {(j, t): P.region(f"hid{j}_{t}") for j in range(FC) for t in range(len(TT_B))}
        x_regs = {(j, t): P.region(f"x{j}_{t}") for j in range(KC) for t in range(len(TT_B))}

        stage_rmsnorm(cx, xv, TT_A, "mixn0", "n0")
        stage_pw1(cx, w_pw1.ap(), uv, u_regs)
        stage_conv_ln(cx, uv, u_regs)
        fine = cx.acta_tiles_fine[1]
        if mode == "l0b":
            vdbg = nc.dram_tensor("vdbg", [D, ACT_COLS], BF16, kind="ExternalOutput")
            dd = _dma(P, "sp", dview(vdbg, KC), cx.acta, fine, [P.region("vdbg")], "st")
            P.wait_ops("sp", [dd])
            P.emit(nc, stack)
            return nc
        cx.work_off = OFF_WORK
        stage_proj_res(cx, w_pw2.ap(), KC, 128, KC, [(s, l, s) for (s, l) in TT_B], lambda t: fine, "b_pw2",
                       xv, x_regs, hv, h_regs, "pw2")
        if mode != "l0a":
            stage_rmsnorm(cx, hv, TT_B, "ffnn0", "n1", src_regs=lambda t: [h_regs[(j, t)] for j in range(KC)])
            stage_gate_up(cx, w_gu0.ap(), hidv, hid_regs, TT_B)
            stage_down(cx, w_dn0.ap(), hidv, hid_regs, hv, h_regs, TT_B)
        if not full:
            P.wait_ops("sp", [r.last_w for r in h_regs.values()])
            P.emit(nc, stack)
            return nc

        if mode == "dbg_l0i":
            hd = nc.dram_tensor("hdump", [D, NTB], F32, kind="ExternalOutput")
            dd = _dma(P, "sp", hd.ap(), h_scr.ap(), list(h_regs.values()), [P.region("hd")], "st")
            P.wait_ops("sp", [dd])
            P.emit(nc, stack)
            return nc
        l1_setup(cx, cmat.ap())
        if mode == "dbg_l1s":
            hd = nc.dram_tensor("hdump", [D, NTB], F32, kind="ExternalOutput")
            dd = _dma(P, "sp", hd.ap(), h_scr.ap(), list(h_regs.values()), [P.region("hd")], "st")
            P.wait_ops("sp", [dd, cx.L.cb_r.last_w, cx.L.nA_r.last_w])
            P.emit(nc, stack)
            return nc
        stage_rmsnorm(cx, hv, TT_B, "mixn1", "n2", src_regs=lambda t: [h_regs[(j, t)] for j in range(KC)],
                      work_off=cx.L.off_end, nbuf=2)
        if mode == "dbg_n2":
            vdbg = nc.dram_tensor("vdbg", [D, ACT_COLS], BF16, kind="ExternalOutput")
            dd = _dma(P, "sp", dview(vdbg, KC), cx.acta, cx.acta_tiles, [P.region("vdbg")], "st")
            P.wait_ops("sp", [dd, cx.L.cb_r.last_w, cx.L.nA_r.last_w])
            P.emit(nc, stack)
            return nc
        stage_gates(cx, w_gate.ap())
        if mode == "dbg_g":
            gd = nc.dram_tensor("gdump", [5, 128, NCK * 64], F32, kind="ExternalOutput")
            L = cx.L
            dd = []
            for i, (a, r) in enumerate([(L.gam, L.gam_r), (L.beta, L.beta_r), (L.bg, L.bg_r), (L.ekg, L.ekg_r), (L.dl, L.dl_r)]):
                dd.append(_dma(P, "sp", gd.ap()[i], a.rearrange("p a n -> p (a n)"), [r], [P.region(f"gd{i}")], "st"))
            P.wait_ops("sp", dd)
            P.emit(nc, stack)
            return nc
        praw = dview(praw_t, NPROJ)
        praw_regs = [P.region(f"praw{j}") for j in range(NPROJ)]
        stage_inproj(cx, w_in.ap(), praw, praw_regs)
        if mode == "dbg_ip":
            gd = nc.dram_tensor("gdump", [5, 128, NCK * 64], F32, kind="ExternalOutput")
            L = cx.L
            dd = []
            for i, (a, r) in enumerate([(L.gam, L.gam_r), (L.beta, L.beta_r), (L.bg, L.bg_r), (L.ekg, L.ekg_r), (L.dl, L.dl_r)]):
                dd.append(_dma(P, "sp", gd.ap()[i], a.rearrange("p a n -> p (a n)"), [r], [P.region(f"gd{i}")], "st"))
            P.wait_ops("sp", dd + [r.last_w for r in praw_regs])
            P.emit(nc, stack)
            return nc
        scr = {
            "qk": [qk_t.ap()[k] for k in range(NKH)], "qk_r": [P.region(f"qks{k}") for k in range(NKH)],
            "ktm": [ktm_t.ap()[k] for k in range(NKH)], "ktm_r": [P.region(f"ktms{k}") for k in range(NKH)],
            "vtm": [vtm_t.ap()[k] for k in range(NKH)], "vtm_r": [P.region(f"vtms{k}") for k in range(NKH)],
            "o": [o_t.ap()[k].rearrange("p (u n) -> p u n", u=2) for k in range(NKH)],
            "o_r": [P.region(f"os{k}") for k in range(NKH)],
            "st": [st_t.ap().rearrange("(k p) n -> k p n", p=128)[k] for k in range(NKH)], "st_r": P.region("sts"),
            "stall": stall_t.ap().rearrange("(s k p) n -> p s k n", s=2, p=128), "stall_r": P.region("stall"),
        }
        J = l1_alloc_job(cx, OFF_ACTA)
        assert J.off_end <= OFF_L1C, J.off_end
        G = l1_alloc_group(cx, cx.L.off_end, "P")
        nkh = 2 if mode in ("dbg_P", "dbg_X", "dbg_S") else NKH
        phase_P(cx, J, G, praw, praw_regs, scr, nkh=nkh)
        if mode == "dbg_P":
            std = nc.dram_tensor("st_dump", [128, 256], F32, kind="ExternalOutput")
            d1 = _dma(P, "sp", std.ap(), J.S.rearrange("p u n -> p (u n)"), [J.S_r], [P.region("std")], "st")
            P.wait_ops("sp", [d1, scr["o_r"][0].last_w, scr["qk_r"][0].last_w, scr["ktm_r"][0].last_w, scr["vtm_r"][0].last_w])
            P.emit(nc, stack)
            return nc
        P.dma("pool", lambda e: e.collective_compute(
            "AllGather", ALU.bypass, replica_groups=[[2 * i, 2 * i + 1] for i in range(ncores // 2)],
            ins=[st_t.ap()], outs=[stall_t.ap()]), [scr["st_r"]], [scr["stall_r"]], "cc", inc=1)
        G2 = l1_alloc_group(cx, cx.L.off_end, "S")
        ogv = dview(og_t, 32)
        og_regs = [P.region(f"og{h}") for h in range(32)]
        if mode == "dbg_X":
            sd = nc.dram_tensor("stall_dump", [2 * NKH * 128, 256], F32, kind="ExternalOutput")
            dd = _dma(P, "sp", sd.ap(), stall_t.ap(), [scr["stall_r"]], [P.region("sd")], "st")
            P.wait_ops("sp", [dd])
            P.emit(nc, stack)
            return nc
        phase_S(cx, J, G2, praw, praw_regs, scr, ogv, og_regs, nkh=nkh)
        if mode == "dbg_S":
            P.wait_ops("sp", [og_regs[i].last_w for i in range(4)])
            P.emit(nc, stack)
            return nc
        if mode == "dbg_og":
            P.wait_ops("sp", [r.last_w for r in og_regs])
        hm_regs = {(j, t): h_regs[(j, t)] for j in range(KC) for t in range(4)}
        stage_down(cx, w_out.ap(), ogv, {(j, t): og_regs[j] for j in range(32) for t in range(4)}, hv, hm_regs, TT_M,
                   FC=32, tag="op")
        stage_rmsnorm(cx, hv, TT_M, "ffnn1", "n3", src_regs=lambda t: [h_regs[(j, t)] for j in range(KC)])
        stage_gate_up(cx, w_gu1.ap(), hidv, hid_regs, TT_M)
        stage_down(cx, w_dn1.ap(), hidv, hid_regs, hv, hm_regs, TT_M, tag="dn1_")
        out_regs = [P.region(f"out{t}") for t in range(4)]
        stage_rmsnorm(cx, hv, TT_M, "finn", "n4", src_regs=lambda t: [h_regs[(j, t)] for j in range(KC)],
                      out_dst=dview(out_t, KC), out_regs=out_regs)
        P.wait_ops("sp", [r.last_w for r in out_regs])
        P.emit(nc, stack)
    return nc


def _blk(w, bw):
    K, N = w.shape
    kc = K // 128
    return np.ascontiguousarray(w.reshape(kc, 128, N // bw, bw).transpose(2, 1, 0, 3)).reshape(N // bw, 128, kc * bw)


def _blk_pair(w, half, bw=128):
    K = w.shape[0]
    kc = K // 128
    a = w[:, :half].reshape(kc, 128, half // bw, bw)
    b = w[:, half:].reshape(kc, 128, half // bw, bw)
    ab = np.concatenate([a, b], axis=3)
    return np.ascontiguousarray(ab.transpose(2, 1, 0, 3)).reshape(half // bw, 128, kc * 2 * bw)


def _cols(v):
    return np.ascontiguousarray(v.reshape(-1, 128).T)


def host_prep(inp):
    f = np.float32
    shared = {}
    shared["w_pw1"] = _blk_pair(np.asarray(inp["cv_w_pw1"][0], f), D)
    shared["w_pw2"] = _blk(np.asarray(inp["cv_w_pw2"][0], f), 128)
    shared["w_gu0"] = _blk_pair(np.asarray(inp["ffn_w_gate_up"][0], f), FFH)
    shared["w_dn0"] = _blk(np.asarray(inp["ffn_w_down"][0], f), 128)
    shared["w_in"] = _blk(np.asarray(inp["dn_w_in"][0][:, :NPROJ * 128], f), 128)
    shared["w_out"] = _blk(np.asarray(inp["dn_w_out"][0], f), 128)
    shared["w_gu1"] = _blk_pair(np.asarray(inp["ffn_w_gate_up"][1], f), FFH)
    shared["w_dn1"] = _blk(np.asarray(inp["ffn_w_down"][1], f), 128)
    shared["cmat"] = host_cmat()
    wg = np.asarray(inp["dn_w_in"][0][:, NPROJ * 128:], f)
    wgs = [wg, np.concatenate([wg[:, 32:64], wg[:, 0:32], wg[:, 96:128], wg[:, 64:96]], axis=1)]
    wgs = [_blk(w, 128)[0] for w in wgs]
    pvs = []
    for par in range(2):
        pv = np.zeros((128, NPV), f)

        def put(name, arr):
            pv[:, PV[name]:PV[name] + arr.shape[1]] = arr
        put("mixn0", _cols(inp["mix_norm"][0])); put("ffnn0", _cols(inp["ffn_norm"][0]))
        put("mixn1", _cols(inp["mix_norm"][1])); put("ffnn1", _cols(inp["ffn_norm"][1]))
        put("finn", _cols(inp["final_norm"]))
        put("b_pw1", _cols(inp["cv_b_pw1"][0])); put("b_dw", _cols(inp["cv_b_dw"][0]))
        put("ln_g", _cols(inp["cv_ln_g"][0])); put("ln_b", _cols(inp["cv_ln_b"][0]))
        put("b_pw2", _cols(inp["cv_b_pw2"][0]))
        wdw = np.asarray(inp["cv_w_dw"][0], f)
        if par:
            wdw = wdw[::-1]
        put("w_dw", np.ascontiguousarray(wdw.T.reshape(KC, 128, 31).transpose(1, 0, 2)).reshape(128, KC * 31))
        wsc = np.asarray(inp["dn_w_conv"][0], f)
        if par:
            wsc = wsc[::-1]
        put("w_sc", np.ascontiguousarray(wsc.T.reshape(64, 128, 5).transpose(1, 0, 2)).reshape(128, 64 * 5))
        put("dnw", np.asarray(inp["dn_norm_w"][0], f).reshape(128, 1))
        al = np.asarray(inp["dn_a_log"][0], f)
        dtb = np.asarray(inp["dn_dt_bias"][0], f)
        order = [1, 0] if par else [0, 1]
        pv[64:128, PV["alog"]] = np.concatenate([al[order[0]], al[order[1]]])
        pv[64:128, PV["dtb"]] = np.concatenate([dtb[order[0]], dtb[order[1]]])
        pv[:, PV["msk"]] = 1.0 if par else 0.0
        pv[:, PV["msk"] + 1] = 0.0 if par else 1.0
        pvs.append(pv)
    x = np.asarray(inp["x"], f)
    in_maps = []
    for c in range(8):
        b, par = c // 2, c % 2
        xs = x[b]
        if par:
            xs = xs[::-1]
        m = dict(shared)
        m["xT"] = np.ascontiguousarray(xs[:NTA].T)
        m["pv"] = pvs[par]
        m["w_gate"] = wgs[par]
        in_maps.append(m)
    return in_maps


_NC_CACHE = {}


def kernel(**inputs):
    if "full" not in _NC_CACHE:
        _NC_CACHE["full"] = build_program("full")
    nc = _NC_CACHE["full"]
    in_maps = host_prep(inputs)
    names = {"xT", "pv", "w_pw1", "w_pw2", "w_gu0", "w_dn0", "cmat", "w_in", "w_gate", "w_out", "w_gu1", "w_dn1"}
    in_maps = [{k: v for k, v in m.items() if k in names} for m in in_maps]
    res = run_bass_kernel_spmd(nc, in_maps, core_ids=list(range(8)))
    out = np.empty((BATCH, SEQ, D), np.float32)
    for c in range(8):
        b, par = c // 2, c % 2
        y = res.results[c]["out"].T
        if par:
            out[b, SEQ - TOK:] = y[::-1]
        else:
            out[b, :TOK] = y
    return out


NH = 32
NKH = 16
CH = 128
NCK = TOK // CH
VD = 4096
CM = {"ident": 0, "trif": 128, "trir": 256, "negf": 384, "negr": 512, "nstf": 640, "nstr": 768}
NCM = 896
OFF_L1C = 8192 + SZ_ACTA


def host_cmat():
    j = np.arange(128)[:, None]
    i = np.arange(128)[None, :]
    m = np.zeros((128, NCM), np.float32)
    m[:, 0:128] = (i == j)
    m[:, 128:256] = (j <= i)
    m[:, 256:384] = (j >= i)
    m[:, 384:512] = np.where(i >= j, 0.0, -30000.0)
    m[:, 512:640] = np.where(i <= j, 0.0, -30000.0)
    m[:, 640:768] = np.where(i > j, -1.0, 0.0)
    m[:, 768:896] = np.where(i < j, -1.0, 0.0)
    return m


def _bc(ap, n):
    return ap.to_broadcast([128, n])


class L1:
    pass


def l1_setup(cx, cmat_dram):
    P = cx.P
    L = L1()
    cx.L = L
    o = OFF_L1C
    L.cm, L.cm_r = cx.sb("cm", o, NCM * 4, F32); o += NCM * 4
    L.cb, L.cb_r = cx.sb("cmb", o, 3 * 128 * 2, BF16); o += 3 * 128 * 2
    L.ones1, L.ones1_r = cx.sb("ones1", o, 256, BF16); o += 256
    L.ones128, L.ones128_r = cx.sb("ones128", o, 256, BF16); o += 256
    L.nA, L.nA_r = cx.sb("nA", o, 64, F32); o += 64
    _dma(P, "sp", L.cm, cmat_dram, [], [L.cm_r], "ld")
    P.op("dve", lambda e: e.tensor_copy(out=L.cb, in_=L.cm[:, 0:384]), [L.cm_r], [L.cb_r])
    P.op("pool", lambda e: e.memset(L.ones1, 1.0), [], [L.ones1_r])
    P.op("pool", lambda e: e.memset(L.ones128, 1.0 / 128), [], [L.ones128_r])
    _act(P, L.nA[:, 0:1], pvcol(cx, "alog"), AF.Exp, [cx.pv_r], [L.nA_r])
    P.op("dve", lambda e: e.tensor_scalar(out=L.nA[:, 0:1], in0=L.nA[:, 0:1], scalar1=-1.0, scalar2=None,
                                          op0=ALU.mult), [L.nA_r], [L.nA_r])
    L.identb = L.cb[:, 0:128]
    L.trib = [L.cb[:, 128:256], L.cb[:, 256:384]]
    L.identf = L.cm[:, 0:128]
    L.neg = [L.cm[:, CM["negf"]:CM["negf"] + 128], L.cm[:, CM["negr"]:CM["negr"] + 128]]
    L.nst = [L.cm[:, CM["nstf"]:CM["nstf"] + 128], L.cm[:, CM["nstr"]:CM["nstr"] + 128]]
    def g(name, dt, sz):
        nonlocal o
        a, r = cx.sb(name, o, NCK * 64 * sz, dt, (NCK,)); o += NCK * 64 * sz
        return a, r
    L.beta, L.beta_r = g("g_beta", F32, 4)
    L.gam, L.gam_r = g("g_gam", F32, 4)
    L.bg, L.bg_r = g("g_bg", F32, 4)
    L.ekg, L.ekg_r = g("g_ekg", F32, 4)
    L.dl, L.dl_r = g("g_dl", F32, 4)
    L.ghi, L.ghi_r = g("g_hi", BF16, 2)
    L.glo, L.glo_r = g("g_lo", BF16, 2)
    L.bb, L.bb_r = g("g_bb", BF16, 2)
    L.off_end = o


def stage_gates(cx, wg_dram):
    P, L = cx.P, cx.L
    o = L.off_end
    wb, wb_r = cx.sb("gt_w", o, KC * 128 * 2, BF16, (KC,)); o += KC * 128 * 2
    gb, gb_r = cx.sb("gt_gb", o, TOK * 4, F32); o += TOK * 4
    gc, gc_r = cx.sb("gt_gc", o, NCK * 128 * 4, F32, (NCK,)); o += NCK * 128 * 4
    tmp, tmp_r = cx.sb("gt_tmp", o, NCK * 64 * 4, F32, (NCK,)); o += NCK * 64 * 4
    _dma(P, "pool", wb, wg_dram.rearrange("p (k n) -> p k n", k=KC), [], [wb_r], "wld")
    gb_t = [P.region(f"gt_gb{t}", parent=gb_r) for t in range(4)]
    for t, (s, l) in enumerate(TT_M):
        ps, ps_r = cx.bank(t % 2)
        _mm(P, ps[:, :l], [(wb[:, kc, :], cx.acta[:, kc, s:s + l]) for kc in range(KC)], [wb_r, cx.acta_tiles[t]], [ps_r])
        _act(P, gb[0:64, s:s + l], ps[0:64, :l], AF.Sigmoid, [ps_r], [gb_t[t]])
        _act(P, gb[64:128, s:s + l], ps[64:128, :l], AF.Exp, [ps_r, cx.pv_r], [gb_t[t]], bias=cx.pv[64:128, PV["dtb"]:PV["dtb"] + 1], scale=1.0)
    for t, (s, l) in enumerate(TT_M):
        _act(P, gb[64:128, s:s + l], gb[64:128, s:s + l], AF.Ln, [gb_t[t], cx.misc_r], [gb_t[t]], bias=cx.misc[64:128, 2:3], scale=1.0)
        P.op("dve", lambda e, s=s, l=l: e.tensor_scalar(out=gb[64:128, s:s + l], in0=gb[64:128, s:s + l],
                                                        scalar1=L.nA[64:128, 0:1], scalar2=None, op0=ALU.mult),
             [gb_t[t], L.nA_r], [gb_t[t]])
    for c in range(NCK):
        ps, ps_r = cx.bank(2 + c % 2)
        P.op("pe", lambda e, c=c, ps=ps: e.transpose(ps[:, 0:128], gb[:, c * 128:(c + 1) * 128], L.identf),
             [gb_t[c // 4], L.cm_r], [ps_r])
        if c % 2 == 0:
            _act(P, gc[:, c, :], ps[:, 0:128], AF.Copy, [ps_r], [gc_r])
        else:
            P.op("dve", lambda e, c=c, ps=ps: e.tensor_copy(out=gc[:, c, :], in_=ps[:, 0:128]), [ps_r], [gc_r])
    P.op("dve", lambda e: e.tensor_copy(out=L.beta, in_=gc[:, :, 0:64]), [gc_r], [L.beta_r])
    P.op("dve", lambda e: e.tensor_copy(out=L.bb, in_=gc[:, :, 0:64]), [gc_r], [L.bb_r])
    P.op("dve", lambda e: e.tensor_copy(out=L.ghi, in_=gc[:, :, 64:128]), [gc_r], [L.ghi_r])
    P.op("dve", lambda e: e.tensor_copy(out=tmp, in_=L.ghi), [L.ghi_r], [tmp_r])
    P.op("dve", lambda e: e.tensor_tensor(out=tmp, in0=gc[:, :, 64:128], in1=tmp, op=ALU.subtract), [gc_r, tmp_r], [tmp_r])
    P.op("dve", lambda e: e.tensor_copy(out=L.glo, in_=tmp), [tmp_r], [L.glo_r])
    for c in range(NCK):
        ps, ps_r = cx.bank(4 + c % 2)
        def fn(e, c=c, ps=ps):
            ins = None
            for d in range(2):
                for k, src in enumerate((L.ghi, L.glo)):
                    ins = e.matmul(ps[:, d * 32:(d + 1) * 32], lhsT=L.trib[d], rhs=src[:, c, d * 32:(d + 1) * 32],
                                   start=(k == 0), stop=(k == 1))
            for k, src in enumerate((L.ghi, L.glo)):
                ins = e.matmul(ps[:, 64:128], lhsT=L.ones1, rhs=src[:, c, :], start=(k == 0), stop=(k == 1))
            return ins
        P.op("pe", fn, [L.cb_r, L.ghi_r, L.glo_r, L.ones1_r], [ps_r])
        P.op("dve", lambda e, c=c, ps=ps: e.tensor_copy(out=L.gam[:, c, :], in_=ps[:, 0:64]), [ps_r], [L.gam_r])
        P.op("dve", lambda e, c=c, ps=ps: e.tensor_tensor(out=L.ekg[:, c, :], in0=ps[:, 64:128], in1=L.gam[:, c, :],
                                                           op=ALU.subtract), [ps_r, L.gam_r], [L.ekg_r])
        _act(P, L.dl[:, c, :], ps[:, 64:128], AF.Exp, [ps_r], [L.dl_r])
    _act(P, L.ekg, L.ekg, AF.Exp, [L.ekg_r], [L.ekg_r])
    _act(P, L.bg, L.gam, AF.Exp, [L.gam_r], [L.bg_r])
    P.op("dve", lambda e: e.tensor_tensor(out=L.bg, in0=L.bg, in1=L.beta, op=ALU.mult), [L.bg_r, L.beta_r], [L.bg_r])
    L.off_end2 = L.off_end


def l1_alloc_job(cx, off):
    P = cx.P
    J = L1()
    o = off

    def mk(name, dt, cols, sz, npar):
        nonlocal o
        aps, regs = [], []
        for p in range(npar):
            a, r = cx.sb(f"{name}{p}", o, cols * sz, dt); o += cols * sz
            if cols == 256:
                a = a.rearrange("p (u n) -> p u n", u=2)
            elif cols == 512:
                a = a.rearrange("p (a u n) -> p a u n", a=2, u=2)
            aps.append(a)
            regs.append(r)
        return aps, regs
    J.E, J.E_r = mk("jE", F32, 256, 4, 3)
    J.T1, J.T1_r = mk("jT1", F32, 256, 4, 3)
    J.Er, J.Er_r = mk("jEr", F32, 256, 4, 3)
    J.KKm, J.KKm_r = mk("jKKm", F32, 128, 4, 3)
    J.NN, J.NN_r = mk("jNN", F32, 512, 4, 3)
    J.Qb, J.Qb_r = mk("jQb", F32, 256, 4, 3)
    J.vb, J.vb_r = mk("jvb", BF16, 256, 2, 3)
    J.kb, J.kb_r = mk("jkb", BF16, 256, 2, 3)
    J.Qh, J.Qh_r = mk("jQh", BF16, 256, 2, 3)
    J.U, J.U_r = mk("jU", F32, 256, 4, 6)
    J.AT, J.AT_r = mk("jAT", BF16, 256, 2, 6)
    J.qg, J.qg_r = mk("jqg", BF16, 256, 2, 6)
    J.kg, J.kg_r = mk("jkg", BF16, 256, 2, 6)
    J.WT, J.WT_r = mk("jWT", BF16, 256, 2, 6)
    a, J.S_r = cx.sb("jS", o, 1024, F32); o += 1024
    J.S = a.rearrange("p (u n) -> p u n", u=2)
    a, J.Sbf_r = cx.sb("jSbf", o, 512, BF16); o += 512
    J.Sbf = a.rearrange("p (u n) -> p u n", u=2)
    a, J.vn_r = cx.sb("jvn", o, 512, BF16); o += 512
    J.vn = a.rearrange("p (u n) -> p u n", u=2)
    J.off_end = o
    return J


def delta_job(cx, J, d, kh, G, o_sink, extra=None, extra_steps=0):
    P, L = cx.P, cx.L
    order = list(range(NCK)) if d == 0 else list(range(NCK - 1, -1, -1))
    uc = [d * 32 + 2 * kh, d * 32 + 2 * kh + 1]

    def u3(ap):
        return ap.rearrange("p (u n) -> p u n", u=2)

    def pre(c, p, pb, sb):
        cs = slice(c * CH, (c + 1) * CH)
        bkA, bkB = cx.banks[2 * p], cx.banks[2 * p + 1]
        rA, rB = cx.bank_reg[2 * p], cx.bank_reg[2 * p + 1]
        psA, psB = bkA[:, :], bkB[:, :]
        E, T1, Er, KKm = J.E[pb], J.T1[pb], J.Er[pb], J.KKm[pb]
        NN, Qb, vb, kb = J.NN[pb], J.Qb[pb], J.vb[pb], J.kb[pb]
        Nb, NTb = NN[:, 0], NN[:, 1]
        NN_r, Qb_r = J.NN_r[pb], J.Qb_r[pb]

        def s1(e):
            e.matmul(psA[:, 0:128], lhsT=G.kT[:, cs], rhs=G.kT[:, cs], start=True, stop=True)
            ins = e.matmul(psA[:, 128:256], lhsT=G.kT[:, cs], rhs=G.qT[:, cs], start=True, stop=True)
            for u in range(2):
                e.matmul(psA[:, 256 + u * 128:384 + u * 128], lhsT=_bc(L.ghi[:, c, uc[u]:uc[u] + 1], 128),
                         rhs=L.trib[d], start=True, stop=False)
                e.matmul(psA[:, 256 + u * 128:384 + u * 128], lhsT=_bc(L.glo[:, c, uc[u]:uc[u] + 1], 128),
                         rhs=L.trib[d], start=False, stop=True)
                ins = e.matmul(psB[:, u * 128:(u + 1) * 128], lhsT=_bc(L.bb[:, c, uc[u]:uc[u] + 1], 128),
                               rhs=L.identb, start=True, stop=True)
            return ins
        P.op("pe", s1, [G.qk_r, L.ghi_r, L.glo_r, L.bb_r, L.cb_r], [rA, rB])
        yield
        for u in range(2):
            P.op("dve", lambda e, u=u: e.scalar_tensor_tensor(
                out=E[:, u, :], in0=psA[:, 256 + u * 128:384 + u * 128], scalar=L.gam[:, c, uc[u]:uc[u] + 1],
                in1=L.neg[d], op0=ALU.subtract, op1=ALU.add), [rA, L.gam_r, L.cm_r], [J.E_r[pb]])
        P.op("dve", lambda e: e.tensor_tensor(out=KKm, in0=psA[:, 0:128], in1=L.nst[d], op=ALU.mult),
             [rA, L.cm_r], [J.KKm_r[pb]])
        yield
        _act(P, Er, u3(psA[:, 256:512]), AF.Exp, [rA], [J.Er_r[pb]])
        _act(P, E, E, AF.Exp, [J.E_r[pb]], [J.E_r[pb]])
        for u in range(2):
            _act(P, vb[:, u, :], G.v_tm[:, c, u, :], AF.Identity, [G.vtm_r, L.beta_r], [J.vb_r[pb]],
                 scale=L.beta[:, c, uc[u]:uc[u] + 1])
            _act(P, kb[:, u, :], G.k_tm[:, c, :], AF.Identity, [G.ktm_r, L.bg_r], [J.kb_r[pb]],
                 scale=L.bg[:, c, uc[u]:uc[u] + 1])
            _act(P, J.kg[sb][:, u, :], G.k_tm[:, c, :], AF.Identity, [G.ktm_r, L.ekg_r], [J.kg_r[sb]],
                 scale=L.ekg[:, c, uc[u]:uc[u] + 1])
        yield
        for u in range(2):
            P.op("dve", lambda e, u=u: e.tensor_tensor(out=J.AT[sb][:, u, :], in0=psA[:, 128:256], in1=E[:, u, :],
                                                       op=ALU.mult), [rA, J.E_r[pb]], [J.AT_r[sb]])
        P.op("dve", lambda e: e.tensor_tensor(out=T1, in0=u3(psB[:, 0:256]), in1=E, op=ALU.mult),
             [rB, J.E_r[pb]], [J.T1_r[pb]])
        yield
        for u in range(2):
            P.op("pool", lambda e, u=u: e.tensor_tensor(out=NTb[:, u, :], in0=T1[:, u, :], in1=KKm,
                                                        op=ALU.mult), [J.T1_r[pb], J.KKm_r[pb]], [NN_r])
        for u in range(2):
            P.op("pool", lambda e, u=u: e.tensor_tensor(out=Qb[:, u, :], in0=NTb[:, u, :], in1=L.identf,
                                                        op=ALU.add), [NN_r, L.cm_r], [Qb_r])
        for u in range(2):
            P.op("pool", lambda e, u=u: e.tensor_tensor(out=J.qg[sb][:, u, :], in0=Er[:, u, :], in1=G.qT[:, cs],
                                                        op=ALU.mult), [J.Er_r[pb], G.qk_r], [J.qg_r[sb]])
        yield

        def s5(e):
            ins = None
            for u in range(2):
                ins = e.transpose(psB[:, u * 128:(u + 1) * 128], NTb[:, u, :], L.identf)
            return ins
        P.op("pe", s5, [NN_r, L.cm_r], [rB])
        yield
        _act(P, Nb, u3(psB[:, 0:256]), AF.Copy, [rB], [NN_r])
        yield
        for k in range(6):
            last = (k == 5)

            def sq(e):
                ins = None
                for u in range(2):
                    ins = e.matmul(psB[:, u * 128:(u + 1) * 128], lhsT=NTb[:, u, :], rhs=Nb[:, u, :],
                                   start=True, stop=True)
                return ins
            P.op("pe", sq, [NN_r], [rB])
            yield
            _act(P, Nb, u3(psB[:, 0:256]), AF.Copy, [rB], [NN_r])
            yield

            def qq(e, last=last):
                ins = None
                for u in range(2):
                    ins = e.matmul(psA[:, u * 128:(u + 1) * 128], lhsT=Nb[:, u, :], rhs=Qb[:, u, :],
                                   start=True, stop=True)
                if not last:
                    for u in range(2):
                        ins = e.transpose(psB[:, 256 + u * 128:384 + u * 128], Nb[:, u, :], L.identf)
                return ins
            P.op("pe", qq, [NN_r, Qb_r, L.cm_r], [rA] if last else [rA, rB])
            yield
            if last:
                P.op("dve", lambda e: e.tensor_tensor(out=J.Qh[pb], in0=u3(psA[:, 0:256]), in1=Qb, op=ALU.add),
                     [rA, Qb_r], [J.Qh_r[pb]])
            else:
                P.op("dve", lambda e: e.tensor_tensor(out=Qb, in0=u3(psA[:, 0:256]), in1=Qb, op=ALU.add),
                     [rA, Qb_r], [Qb_r])
            if not last:
                _act(P, NTb, u3(psB[:, 256:512]), AF.Copy, [rB], [NN_r])
            yield

        def fin(e):
            ins = None
            for u in range(2):
                e.matmul(psB[:, u * 128:(u + 1) * 128], lhsT=J.Qh[pb][:, u, :], rhs=vb[:, u, :], start=True, stop=True)
                ins = e.matmul(psB[:, 256 + u * 128:384 + u * 128], lhsT=kb[:, u, :], rhs=J.Qh[pb][:, u, :],
                               start=True, stop=True)
            return ins
        P.op("pe", fin, [J.Qh_r[pb], J.vb_r[pb], J.kb_r[pb]], [rB])
        yield
        _act(P, J.U[sb], u3(psB[:, 0:256]), AF.Copy, [rB], [J.U_r[sb]])
        _act(P, J.WT[sb], u3(psB[:, 256:512]), AF.Copy, [rB], [J.WT_r[sb]])
        yield

    pS, rS = cx.banks[6][:, 0:256], cx.bank_reg[6]
    pO, rO = cx.banks[6][:, 256:512], cx.bank_reg[6]
    pD, rD = cx.banks[7][:, 0:256], cx.bank_reg[7]

    def scan(c, sb):
        def ws(e):
            ins = None
            for u in range(2):
                ins = e.matmul(pS[:, u * 128:(u + 1) * 128], lhsT=J.WT[sb][:, u, :], rhs=J.Sbf[:, u, :], start=True, stop=True)
            return ins
        P.op("pe", ws, [J.WT_r[sb], J.Sbf_r], [rS])
        yield
        P.op("dve", lambda e: e.tensor_tensor(out=J.vn, in0=J.U[sb], in1=u3(pS), op=ALU.subtract),
             [J.U_r[sb], rS], [J.vn_r])
        yield

        def oo(e):
            ins = None
            for u in range(2):
                e.matmul(pO[:, u * 128:(u + 1) * 128], lhsT=J.Sbf[:, u, :], rhs=J.qg[sb][:, u, :], start=True, stop=False)
                e.matmul(pO[:, u * 128:(u + 1) * 128], lhsT=J.vn[:, u, :], rhs=J.AT[sb][:, u, :], start=False, stop=True)
            for u in range(2):
                ins = e.matmul(pD[:, u * 128:(u + 1) * 128], lhsT=J.kg[sb][:, u, :], rhs=J.vn[:, u, :], start=True, stop=True)
            return ins
        P.op("pe", oo, [J.Sbf_r, J.qg_r[sb], J.vn_r, J.AT_r[sb], J.kg_r[sb]], [rO, rD])
        yield
        for u in range(2):
            P.op("dve", lambda e, u=u: e.scalar_tensor_tensor(
                out=J.S[:, u, :], in0=J.S[:, u, :], scalar=L.dl[:, c, uc[u]:uc[u] + 1], in1=pD[:, u * 128:(u + 1) * 128],
                op0=ALU.mult, op1=ALU.add), [J.S_r, L.dl_r, rD], [J.S_r])
        _act(P, J.Sbf, J.S, AF.Copy, [J.S_r], [J.Sbf_r])
        o_sink(c, u3(pO), rO)
        yield

    def scans(lst):
        for (c, sb) in lst:
            yield from scan(c, sb)

    def interleave(gens):
        gens = [g for g in gens if g is not None]
        while gens:
            alive = []
            for g in gens:
                try:
                    next(g)
                    alive.append(g)
                except StopIteration:
                    pass
            gens = alive

    prev = []
    n = 0
    grp = 0
    while n < NCK:
        cur = order[n:n + 3]
        gens = []
        lst = []
        for i, c in enumerate(cur):
            sb = (3 * (grp % 2) + i)
            gens.append(pre(c, i, i, sb))
            lst.append((c, sb))
        gens.append(scans(prev) if prev else None)
        if extra is not None:
            def limited(g=extra, k=extra_steps):
                for _ in range(k):
                    try:
                        next(g)
                    except StopIteration:
                        return
                    yield
            gens.append(limited())
        interleave(gens)
        prev = lst
        n += len(cur)
        grp += 1
    interleave([scans(prev)])


NPROJ = 96


def stage_inproj(cx, w_dram, praw, praw_regs):
    P = cx.P
    o = cx.L.off_end
    NW = 3
    wb, wb_r = [], []
    for i in range(NW):
        a, r = cx.sb(f"ip_w{i}", o, KC * 128 * 2, BF16, (KC,)); wb.append(a); wb_r.append(r); o += KC * 128 * 2
    st, st_r = [], []
    for i in range(3):
        a, r = cx.sb(f"ip_st{i}", o, 2048, F32); st.append(a); st_r.append(r); o += 2048

    def wload(j):
        _dma(P, "pool", wb[j % NW], w_dram[j].rearrange("p (k n) -> p k n", k=KC), [], [wb_r[j % NW]], "wld")

    wload(0); wload(1)
    n = 0
    for j in range(NPROJ):
        if j + 2 < NPROJ:
            wload(j + 2)
        w = wb[j % NW]
        tiles = TT_B if j < 64 else TT_M
        for t, (s, l) in enumerate(tiles):
            b = n % 3
            ps, ps_r = cx.bank(n % 3)
            _mm(P, ps[:, :l], [(w[:, kc, :], cx.acta[:, kc, s:s + l]) for kc in range(KC)],
                [wb_r[j % NW], cx.acta_tiles[t]], [ps_r])
            if n % 2 == 0:
                _act(P, st[b][:, :l], ps[:, :l], AF.Copy, [ps_r], [st_r[b]])
            else:
                P.op("dve", lambda e, b=b, l=l, ps=ps: e.tensor_copy(out=st[b][:, :l], in_=ps[:, :l]), [ps_r], [st_r[b]])
            _dma(P, "sp", praw[:, j, s:s + l], st[b][:, :l], [st_r[b]], [praw_regs[j]], "st")
            n += 1


def l1_alloc_group(cx, off, phase):
    P = cx.P
    G = L1()
    o = off

    def mk(name, dt, nbytes, shape=None):
        nonlocal o
        a, r = cx.sb(f"G{phase}_{name}", o, nbytes, dt, shape); o += nbytes
        return a, r
    nset = 2 if phase == "P" else 1
    G.sets = []
    for i in range(nset):
        S_ = L1()
        S_.qk, S_.qk_r = mk(f"qk{i}", BF16, 2 * TOK * 2, (2,))
        S_.qT, S_.kT = S_.qk[:, 0, :], S_.qk[:, 1, :]
        S_.k_tm, S_.ktm_r = mk(f"ktm{i}", BF16, NCK * 128 * 2, (NCK,))
        S_.v_tm, S_.vtm_r = mk(f"vtm{i}", BF16, NCK * 256 * 2, (NCK, 2))
        G.sets.append(S_)
    G.qk, G.qk_r, G.qT, G.kT = G.sets[0].qk, G.sets[0].qk_r, G.sets[0].qT, G.sets[0].kT
    G.k_tm, G.ktm_r, G.v_tm, G.vtm_r = G.sets[0].k_tm, G.sets[0].ktm_r, G.sets[0].v_tm, G.sets[0].vtm_r
    if phase == "P":
        G.rw, G.rw_r = mk("rw", F32, (NTB + 6) * 4)
        G.cv, G.cv_r = mk("cv", F32, TOK * 4)
        G.sqb, G.sqb_r = mk("sqb", BF16, TOK * 2)
        G.rn, G.rn_r = mk("rn", F32, 2048)
        G.vT, G.vT_r = mk("vT", BF16, 2 * TOK * 2, (2,))
        G.ost, G.ost_r = [], []
        for i in range(2):
            a, r = mk(f"ost{i}", F32, 1024, (2,)); G.ost.append(a); G.ost_r.append(r)
    else:
        G.ot, G.ot_r = mk("ot", F32, 2 * TOK * 4, (2,))
        G.z, G.z_r = mk("z", F32, 2 * TOK * 4, (2,))
        G.sq, G.sq_r = mk("sq", BF16, 2 * TOK * 2, (2,))
        G.og, G.og_r = mk("og", BF16, 2 * TOK * 2, (2,))
        G.rn, G.rn_r = mk("rn", F32, 2048)
        G.sl, G.sl_r = mk("sl", F32, 2 * 256 * 4, (2,))
    G.off_end = o
    return G


def group_stage_P(cx, G, kh, praw, praw_regs, scr):
    T = G.sets[kh % 2]
    P, L = cx.P, cx.L
    blocks = [(kh, 0), (16 + kh, 1), (32 + 2 * kh, 2), (32 + 2 * kh + 1, 3)]
    QW = TOK // 4
    for (j, b) in blocks:
        _dma(P, "sp", G.rw[:, 2:2 + NTB], praw[:, j, :], [praw_regs[j]], [G.rw_r], "ld")
        for tap in range(5):
            for q in range(4):
                wcol = cx.pv[:, PV["w_sc"] + j * 5 + tap: PV["w_sc"] + j * 5 + tap + 1]
                osl = slice(q * QW, (q + 1) * QW)
                isl = slice(q * QW + tap, (q + 1) * QW + tap)
                if tap == 0:
                    P.op("dve", lambda e, osl=osl, isl=isl, wcol=wcol: e.tensor_scalar(
                        out=G.cv[:, osl], in0=G.rw[:, isl], scalar1=wcol, scalar2=None, op0=ALU.mult),
                        [G.rw_r, cx.pv_r], [G.cv_r] if q == 0 else [])
                else:
                    P.op("dve", lambda e, osl=osl, isl=isl, wcol=wcol: e.scalar_tensor_tensor(
                        out=G.cv[:, osl], in0=G.rw[:, isl], scalar=wcol, in1=G.cv[:, osl], op0=ALU.mult, op1=ALU.add),
                        [G.rw_r] if (tap == 4 and q == 3) else [], [G.cv_r] if (tap == 4 and q == 3) else [])
            yield
        if b >= 2:
            _act(P, G.vT[:, b - 2, :], G.cv, AF.Silu, [G.cv_r], [G.vT_r])
            yield
        else:
            _act(P, G.cv, G.cv, AF.Silu, [G.cv_r], [G.cv_r])
            _act(P, G.sqb, G.cv, AF.Square, [G.cv_r], [G.sqb_r])
            for t in range(8):
                s, l = t * 256, 256
                ps, ps_r = cx.banks[7][:, 256:512], cx.bank_reg[7]
                _mm(P, ps[:, :l], [(L.ones1, G.sqb[:, s:s + l])], [L.ones1_r, G.sqb_r], [ps_r])
                _act(P, G.rn[:, :l], ps[:, :l], AF.Sqrt, [ps_r, cx.misc_r], [G.rn_r], bias=cx.misc[:, 0:1], scale=1.0)
                P.op("dve", lambda e, l=l: e.reciprocal(out=G.rn[:, :l], in_=G.rn[:, :l]), [G.rn_r], [G.rn_r])
                sc = (128.0 ** -0.5) if b == 0 else 1.0
                P.op("dve", lambda e, s=s, l=l, b=b, sc=sc: e.scalar_tensor_tensor(
                    out=T.qk[:, b, s:s + l], in0=G.cv[:, s:s + l], scalar=sc, in1=G.rn[:, :l], op0=ALU.mult, op1=ALU.mult),
                    [G.cv_r, G.rn_r], [T.qk_r])
                if t % 2 == 1:
                    yield
    for c in range(NCK):
        bank = 7
        psb = cx.banks[bank][:, 256:448].bitcast(BF16)
        ps_r = cx.bank_reg[bank]

        def tr(e, c=c, psb=psb):
            cs = slice(c * CH, (c + 1) * CH)
            e.transpose(psb[:, 0:128], T.kT[:, cs], L.identb)
            e.transpose(psb[:, 128:256], G.vT[:, 0, cs], L.identb)
            return e.transpose(psb[:, 256:384], G.vT[:, 1, cs], L.identb)
        P.op("pe", tr, [T.qk_r, G.vT_r, L.cb_r], [ps_r])
        _act(P, T.k_tm[:, c, :], psb[:, 0:128], AF.Copy, [ps_r], [T.ktm_r])
        P.op("dve", lambda e, c=c, psb=psb: e.tensor_copy(out=T.v_tm[:, c, :, :],
                                                          in_=psb[:, 128:384].rearrange("p (u n) -> p u n", u=2)),
             [ps_r], [T.vtm_r])
        if c % 2 == 1:
            yield
    _dma(P, "sp", scr["qk"][kh], T.qk.rearrange("p a n -> p (a n)"), [T.qk_r], [scr["qk_r"][kh]], "st")
    _dma(P, "sp", scr["ktm"][kh], T.k_tm.rearrange("p a n -> p (a n)"), [T.ktm_r], [scr["ktm_r"][kh]], "st")
    _dma(P, "sp", scr["vtm"][kh], T.v_tm.rearrange("p a b n -> p (a b n)"), [T.vtm_r], [scr["vtm_r"][kh]], "st")


def phase_P(cx, J, G, praw, praw_regs, scr, nkh=NKH):
    P = cx.P
    P.op("pool", lambda e: e.memset(G.rw[:, 0:2], 0.0), [], [G.rw_r])
    for _ in group_stage_P(cx, G, 0, praw, praw_regs, scr):
        pass
    for kh in range(nkh):
        nxt = group_stage_P(cx, G, kh + 1, praw, praw_regs, scr) if kh + 1 < nkh else None
        P.op("pool", lambda e: e.memset(J.S, 0.0), [], [J.S_r])
        P.op("pool", lambda e: e.memset(J.Sbf, 0.0), [], [J.Sbf_r])
        cnt = [0]

        def sink(c, psO, rO, kh=kh, cnt=cnt):
            b = cnt[0] % 2
            cnt[0] += 1
            _act(P, G.ost[b], psO, AF.Copy, [rO], [G.ost_r[b]])
            _dma(P, "sp", scr["o"][kh][:, :, c * CH:(c + 1) * CH], G.ost[b], [G.ost_r[b]], [scr["o_r"][kh]], "st")
        delta_job(cx, J, 0, kh, G.sets[kh % 2], sink, extra=nxt, extra_steps=14)
        if nxt is not None:
            for _ in nxt:
                pass
        _dma(P, "sp", scr["st"][kh], J.S.rearrange("p u n -> p (u n)"), [J.S_r], [scr["st_r"]], "st")


def phase_S(cx, J, G, praw, praw_regs, scr, og_scr, og_regs, nkh=NKH):
    P, L = cx.P, cx.L
    for kh in range(nkh):
        _dma(P, "sp", G.qk.rearrange("p a n -> p (a n)"), scr["qk"][kh], [scr["qk_r"][kh]], [G.qk_r], "ld")
        _dma(P, "sp", G.k_tm.rearrange("p a n -> p (a n)"), scr["ktm"][kh], [scr["ktm_r"][kh]], [G.ktm_r], "ld")
        _dma(P, "sp", G.v_tm.rearrange("p a b n -> p (a b n)"), scr["vtm"][kh], [scr["vtm_r"][kh]], [G.vtm_r], "ld")
        _dma(P, "sp", G.ot, scr["o"][kh], [scr["o_r"][kh]], [G.ot_r], "ld")
        for u in range(2):
            j = 64 + 2 * kh + u
            _dma(P, "sp", G.z[:, u, :], praw[:, j, 0:TOK], [praw_regs[j]], [G.z_r], "ld")
        _dma(P, "sp", G.sl, scr["stall"][:, :, kh, :], [scr["stall_r"]], [G.sl_r], "ld")
        Sf = J.S.rearrange("p u n -> p (u n)")
        P.op("dve", lambda e: e.tensor_scalar(out=Sf, in0=G.sl[:, 0, :], scalar1=pvcol(cx, "msk", 0), scalar2=None,
                                              op0=ALU.mult), [G.sl_r, cx.pv_r], [J.S_r])
        P.op("dve", lambda e: e.scalar_tensor_tensor(out=Sf, in0=G.sl[:, 1, :], scalar=pvcol(cx, "msk", 1), in1=Sf,
                                                     op0=ALU.mult, op1=ALU.add), [G.sl_r, cx.pv_r, J.S_r], [J.S_r])
        _act(P, J.Sbf, J.S, AF.Copy, [J.S_r], [J.Sbf_r])

        def sink(c, psO, rO):
            P.op("dve", lambda e, c=c: e.tensor_tensor(out=G.ot[:, :, c * CH:(c + 1) * CH],
                                                       in0=G.ot[:, :, c * CH:(c + 1) * CH], in1=psO, op=ALU.add),
                 [rO, G.ot_r], [G.ot_r])
        delta_job(cx, J, 1, kh, G.sets[0], sink)
        _act(P, G.sq, G.ot, AF.Square, [G.ot_r], [G.sq_r])
        _act(P, G.z, G.z, AF.Silu, [G.z_r], [G.z_r])
        for u in range(2):
            for t, (s, l) in enumerate(TT_M):
                ps, ps_r = cx.bank(t % 2)
                _mm(P, ps[:, :l], [(L.ones128, G.sq[:, u, s:s + l])], [L.ones128_r, G.sq_r], [ps_r])
                _act(P, G.rn[:, :l], ps[:, :l], AF.Sqrt, [ps_r, cx.misc_r], [G.rn_r], bias=cx.misc[:, 0:1], scale=1.0)
                P.op("dve", lambda e, l=l: e.reciprocal(out=G.rn[:, :l], in_=G.rn[:, :l]), [G.rn_r], [G.rn_r])
                P.op("dve", lambda e, u=u, s=s, l=l: e.scalar_tensor_tensor(
                    out=G.ot[:, u, s:s + l], in0=G.ot[:, u, s:s + l], scalar=pvcol(cx, "dnw"), in1=G.rn[:, :l],
                    op0=ALU.mult, op1=ALU.mult), [G.ot_r, G.rn_r, cx.pv_r], [G.ot_r])
                P.op("dve", lambda e, u=u, s=s, l=l: e.tensor_tensor(out=G.og[:, u, s:s + l], in0=G.ot[:, u, s:s + l],
                                                                     in1=G.z[:, u, s:s + l], op=ALU.mult),
                     [G.ot_r, G.z_r], [G.og_r])
            _dma(P, "sp", og_scr[:, 2 * kh + u, :], G.og[:, u, :], [G.og_r], [og_regs[2 * kh + u]], "st")
```
